# Optimizing a Trainium2 kernel written in Bass

```python
import math
import jax, jax.numpy as jnp
from jax import lax
import numpy as np

D_MODEL = 2048
BATCH = 4
SEQ = 2048
DEPTH = 1
DEC_BATCH = 128
DEC_SEQ = 8
PAST_LEN = 16384
PAGE_SIZE = 128

MIX_A = D_MODEL // 2
MIX_B = D_MODEL - MIX_A
DN_HEAD_DIM = 128
DN_HEADS = MIX_A // DN_HEAD_DIM
CONV_WIDTH = 4
CHUNK = 64
S5_GROUP = 16
S5_GROUPS = MIX_B // S5_GROUP
S5_STATE = 64
QKV_DIM = 3 * MIX_A
PROJ_DIM = QKV_DIM + MIX_A + 2 * DN_HEADS + MIX_B
D_FF = ((8 * D_MODEL + 767) // 768) * 256
EPS = 1e-6

kernel_name = 'hybrid_gdn_s5_decoder_step'


def rmsnorm(x, w):
    x32 = x.astype(jnp.float32)
    y = x32 * lax.rsqrt(jnp.mean(x32 * x32, axis=-1, keepdims=True) + EPS)
    return (y * w.astype(jnp.float32)).astype(x.dtype)


def l2norm(x):
    return x * lax.rsqrt(jnp.sum(x * x, axis=-1, keepdims=True) + 1e-6)


def gated_delta_rule(q, k, v, g, beta, s0):
    bsz, t, h, _ = q.shape
    dv = v.shape[-1]
    c = CHUNK if t >= CHUNK else t
    pad = (-t) % c
    if pad:
        pw = ((0, 0), (0, pad), (0, 0), (0, 0))
        q, k, v = jnp.pad(q, pw), jnp.pad(k, pw), jnp.pad(v, pw)
        g, beta = jnp.pad(g, pw[:3]), jnp.pad(beta, pw[:3])
    n = (t + pad) // c

    def chunks(a):
        a = a.reshape((bsz, n, c, h) + a.shape[3:])
        return jnp.moveaxis(a, 3, 2).swapaxes(0, 1)

    qc, kc, vc, gc, bc = [chunks(a) for a in (q, k, v, g, beta)]
    gcum = jnp.cumsum(gc, axis=-1)
    idx = jnp.arange(c)
    causal = idx[:, None] >= idx[None, :]
    strict = idx[:, None] > idx[None, :]
    decay = jnp.exp(jnp.where(causal, gcum[..., :, None] - gcum[..., None, :], -jnp.inf))
    kb = kc * bc[..., None]
    lmat = jnp.where(strict, jnp.einsum('nbhik,nbhjk->nbhij', kb, kc) * decay, 0.0)
    eye = jnp.broadcast_to(jnp.eye(c, dtype=jnp.float32), lmat.shape)
    tinv = lax.linalg.triangular_solve(eye + lmat, eye, left_side=True, lower=True)
    u = jnp.einsum('nbhij,nbhjv->nbhiv', tinv, vc * bc[..., None])
    w = jnp.einsum('nbhij,nbhjk->nbhik', tinv, kb * jnp.exp(gcum)[..., None])
    attn = jnp.einsum('nbhik,nbhjk->nbhij', qc, kc) * decay

    def step(s, xs):
        q_i, k_i, u_i, w_i, a_i, g_i = xs
        v_new = u_i - jnp.einsum('bhck,bhkv->bhcv', w_i, s)
        o = (jnp.einsum('bhck,bhkv->bhcv', q_i * jnp.exp(g_i)[..., None], s)
             + jnp.einsum('bhij,bhjv->bhiv', a_i, v_new))
        g_last = g_i[..., -1]
        s = (s * jnp.exp(g_last)[..., None, None]
             + jnp.einsum('bhck,bhcv->bhkv', k_i * jnp.exp(g_last[..., None] - g_i)[..., None], v_new))
        return s, o

    s_fin, o = lax.scan(step, s0, (qc, kc, u, w, attn, gcum))
    o = jnp.moveaxis(o.swapaxes(0, 1), 2, 3).reshape(bsz, n * c, h, dv)[:, :t]
    return o, s_fin


def s5_ssm(u, lam_re, lam_im, log_step, b_re, b_im, c_re, c_im, d_skip, h0_re, h0_im):
    f32 = jnp.float32
    u = u.astype(f32)
    lre = jnp.minimum(lam_re.astype(f32), -1e-4)
    lim = lam_im.astype(f32)
    dt = jnp.exp(log_step.astype(f32))[:, None]
    mag = jnp.exp(lre * dt)
    ang = lim * dt
    ab_re, ab_im = mag * jnp.cos(ang), mag * jnp.sin(ang)
    nr, ni = ab_re - 1.0, ab_im
    den = lre * lre + lim * lim
    co_re = (nr * lre + ni * lim) / den
    co_im = (ni * lre - nr * lim) / den
    br, bi = b_re.astype(f32), b_im.astype(f32)
    bb_re = co_re[..., None] * br - co_im[..., None] * bi
    bb_im = co_re[..., None] * bi + co_im[..., None] * br
    bu_re = jnp.einsum('btgh,gph->btgp', u, bb_re)
    bu_im = jnp.einsum('btgh,gph->btgp', u, bb_im)
    h0r, h0i = h0_re.astype(f32), h0_im.astype(f32)
    bu_re = bu_re.at[:, 0].add(ab_re * h0r - ab_im * h0i)
    bu_im = bu_im.at[:, 0].add(ab_re * h0i + ab_im * h0r)
    a_re = jnp.broadcast_to(ab_re, bu_re.shape)
    a_im = jnp.broadcast_to(ab_im, bu_im.shape)

    def combine(e1, e2):
        a1r, a1i, b1r, b1i = e1
        a2r, a2i, b2r, b2i = e2
        return (a2r * a1r - a2i * a1i, a2r * a1i + a2i * a1r,
                a2r * b1r - a2i * b1i + b2r, a2r * b1i + a2i * b1r + b2i)

    _, _, h_re, h_im = lax.associative_scan(combine, (a_re, a_im, bu_re, bu_im), axis=1)
    y = (jnp.einsum('ghp,btgp->btgh', c_re.astype(f32), h_re)
         - jnp.einsum('ghp,btgp->btgh', c_im.astype(f32), h_im)
         + d_skip.astype(f32) * u)
    return y, h_re[:, -1], h_im[:, -1]


def decoder_layer(x, c, conv_buf, s0, h0_re, h0_im,
                  w_ada, b_ada, g_pre_mix, g_post_mix, g_pre_ffn, g_post_ffn,
                  w_in, w_conv, a_log, dt_bias, g_dn_out,
                  lam_re, lam_im, log_step, b_re, b_im, c_re, c_im, d_skip,
                  w_glu, g_s5_out, w_out, w_gate, w_up, w_down):
    f32 = jnp.float32
    bsz, t, _ = x.shape
    mod = jax.nn.silu(c) @ w_ada + b_ada
    sh1, sc1, gt1, sh2, sc2, gt2 = [m[:, None, :] for m in jnp.split(mod, 6, axis=-1)]

    h = rmsnorm(x, g_pre_mix) * (1 + sc1) + sh1
    proj = h @ w_in
    cuts = [QKV_DIM, QKV_DIM + MIX_A, QKV_DIM + MIX_A + DN_HEADS, QKV_DIM + MIX_A + 2 * DN_HEADS]
    qkv, z, a_raw, b_raw, u = jnp.split(proj, cuts, axis=-1)

    xcat = jnp.concatenate([conv_buf.astype(qkv.dtype), qkv], axis=1)
    conv = sum(xcat[:, j:j + t] * w_conv[j] for j in range(CONV_WIDTH))
    new_conv = xcat[:, t:]
    qkv = jax.nn.silu(conv.astype(f32))
    q, k, v = [a.reshape(bsz, t, DN_HEADS, DN_HEAD_DIM) for a in jnp.split(qkv, 3, axis=-1)]
    q = l2norm(q) * (DN_HEAD_DIM ** -0.5)
    k = l2norm(k)
    beta = jax.nn.sigmoid(b_raw.astype(f32))
    g = -jnp.exp(a_log.astype(f32)) * jax.nn.softplus(a_raw.astype(f32) + dt_bias.astype(f32))
    o, s_new = gated_delta_rule(q, k, v, g, beta, s0.astype(f32))
    o = rmsnorm(o, g_dn_out) * jax.nn.silu(z.astype(f32).reshape(bsz, t, DN_HEADS, DN_HEAD_DIM))
    o = o.reshape(bsz, t, MIX_A).astype(x.dtype)

    y5, hr, hi = s5_ssm(u.reshape(bsz, t, S5_GROUPS, S5_GROUP), lam_re, lam_im, log_step,
                        b_re, b_im, c_re, c_im, d_skip, h0_re, h0_im)
    y5 = jax.nn.gelu(y5.reshape(bsz, t, MIX_B)).astype(x.dtype)
    ab = y5 @ w_glu
    y5 = ab[..., :MIX_B] * jax.nn.sigmoid(ab[..., MIX_B:])
    y5 = rmsnorm(y5, g_s5_out)

    mix = jnp.concatenate([o, y5], axis=-1) @ w_out
    x = x + gt1 * rmsnorm(mix, g_post_mix)

    h = rmsnorm(x, g_pre_ffn) * (1 + sc2) + sh2
    f = (jax.nn.silu(h @ w_gate) * (h @ w_up)) @ w_down
    x = x + gt2 * rmsnorm(f, g_post_ffn)
    return x, new_conv.astype(x.dtype), s_new.astype(x.dtype), hr.astype(x.dtype), hi.astype(x.dtype)


def setup_inputs(seed: int = 0) -> dict:
    key = jax.random.key(seed)
    ks = iter(jax.random.split(key, 48))
    nrm = lambda shape, s: jax.random.normal(next(ks), shape, jnp.float32) * s
    L = DEPTH
    gain = lambda n: 1.0 + nrm((L, n), 0.01)
    dt = jnp.exp(jax.random.uniform(next(ks), (L, DN_HEADS), jnp.float32, math.log(1e-3), math.log(1e-1)))
    lam_im0 = jnp.broadcast_to(math.pi * jnp.arange(S5_STATE, dtype=jnp.float32), (L, S5_GROUPS, S5_STATE))
    return {
        'x_prompt': nrm((BATCH, SEQ, D_MODEL), 1.0),
        'x_sample': nrm((DEC_BATCH, DEC_SEQ, D_MODEL), 1.0),
        'c_prompt': nrm((BATCH, D_MODEL), 1.0),
        'c_sample': nrm((DEC_BATCH, D_MODEL), 1.0),
        'state_conv': nrm((L, DEC_BATCH, CONV_WIDTH - 1, QKV_DIM), 1.0),
        'state_delta': nrm((L, DEC_BATCH, DN_HEADS, DN_HEAD_DIM, DN_HEAD_DIM), 0.1),
        'state_ssm_re': nrm((L, DEC_BATCH, S5_GROUPS, S5_STATE), 0.5),
        'state_ssm_im': nrm((L, DEC_BATCH, S5_GROUPS, S5_STATE), 0.5),
        'w_ada': nrm((L, D_MODEL, 6 * D_MODEL), D_MODEL ** -0.5),
        'b_ada': nrm((L, 6 * D_MODEL), 0.01),
        'g_pre_mix': gain(D_MODEL),
        'g_post_mix': gain(D_MODEL),
        'g_pre_ffn': gain(D_MODEL),
        'g_post_ffn': gain(D_MODEL),
        'w_in': nrm((L, D_MODEL, PROJ_DIM), D_MODEL ** -0.5),
        'w_conv': nrm((L, CONV_WIDTH, QKV_DIM), CONV_WIDTH ** -0.5),
        'a_log': jnp.log(jax.random.uniform(next(ks), (L, DN_HEADS), jnp.float32, 1.0, 16.0)),
        'dt_bias': dt + jnp.log(-jnp.expm1(-dt)),
        'g_dn_out': gain(DN_HEAD_DIM),
        'lam_re': -0.5 + nrm((L, S5_GROUPS, S5_STATE), 0.01),
        'lam_im': lam_im0 + nrm((L, S5_GROUPS, S5_STATE), 0.01),
        'log_step': jax.random.uniform(next(ks), (L, S5_GROUPS), jnp.float32, math.log(1e-3), math.log(1e-1)),
        'b_re': nrm((L, S5_GROUPS, S5_STATE, S5_GROUP), (2 * S5_GROUP) ** -0.5),
        'b_im': nrm((L, S5_GROUPS, S5_STATE, S5_GROUP), (2 * S5_GROUP) ** -0.5),
        'c_re': nrm((L, S5_GROUPS, S5_GROUP, S5_STATE), (2 * S5_STATE) ** -0.5),
        'c_im': nrm((L, S5_GROUPS, S5_GROUP, S5_STATE), (2 * S5_STATE) ** -0.5),
        'd_skip': nrm((L, S5_GROUPS, S5_GROUP), 1.0),
        'w_glu': nrm((L, MIX_B, 2 * MIX_B), MIX_B ** -0.5),
        'g_s5_out': gain(MIX_B),
        'w_out': nrm((L, D_MODEL, D_MODEL), D_MODEL ** -0.5),
        'w_gate': nrm((L, D_MODEL, D_FF), D_MODEL ** -0.5),
        'w_up': nrm((L, D_MODEL, D_FF), D_MODEL ** -0.5),
        'w_down': nrm((L, D_FF, D_MODEL), D_FF ** -0.5),
    }


def reference(x_prompt, x_sample, c_prompt, c_sample, state_conv, state_delta, state_ssm_re, state_ssm_im,
              w_ada, b_ada, g_pre_mix, g_post_mix, g_pre_ffn, g_post_ffn,
              w_in, w_conv, a_log, dt_bias, g_dn_out,
              lam_re, lam_im, log_step, b_re, b_im, c_re, c_im, d_skip,
              w_glu, g_s5_out, w_out, w_gate, w_up, w_down):
    weights = (w_ada, b_ada, g_pre_mix, g_post_mix, g_pre_ffn, g_post_ffn,
               w_in, w_conv, a_log, dt_bias, g_dn_out,
               lam_re, lam_im, log_step, b_re, b_im, c_re, c_im, d_skip,
               w_glu, g_s5_out, w_out, w_gate, w_up, w_down)
    bp = x_prompt.shape[0]
    dt_ = x_prompt.dtype
    yp, ys = x_prompt, x_sample
    pc, pd, pr, pim, sc, sd, sr, sim = [], [], [], [], [], [], [], []
    for l in range(DEPTH):
        lw = [w[l] for w in weights]
        zc = jnp.zeros((bp, CONV_WIDTH - 1, QKV_DIM), dt_)
        zd = jnp.zeros((bp, DN_HEADS, DN_HEAD_DIM, DN_HEAD_DIM), dt_)
        zs = jnp.zeros((bp, S5_GROUPS, S5_STATE), dt_)
        yp, c1, d1, r1, i1 = decoder_layer(yp, c_prompt, zc, zd, zs, zs, *lw)
        ys, c2, d2, r2, i2 = decoder_layer(ys, c_sample, state_conv[l], state_delta[l],
                                           state_ssm_re[l], state_ssm_im[l], *lw)
        pc.append(c1); pd.append(d1); pr.append(r1); pim.append(i1)
        sc.append(c2); sd.append(d2); sr.append(r2); sim.append(i2)
    conv_prompt, delta_prompt = jnp.stack(pc), jnp.stack(pd)
    ssm_re_prompt, ssm_im_prompt = jnp.stack(pr), jnp.stack(pim)
    conv_sample, delta_sample = jnp.stack(sc), jnp.stack(sd)
    ssm_re_sample, ssm_im_sample = jnp.stack(sr), jnp.stack(sim)
    return (yp, ys, conv_prompt, delta_prompt, ssm_re_prompt, ssm_im_prompt,
            conv_sample, delta_sample, ssm_re_sample, ssm_im_sample)
```

```python
import contextlib
import math
import numpy as np
import concourse.bass as bass
import concourse.mybir as mybir
from concourse.bass_utils import run_bass_kernel_spmd

F32 = mybir.dt.float32
BF16 = mybir.dt.bfloat16
AF = mybir.ActivationFunctionType
ALU = mybir.AluOpType
AX = mybir.AxisListType

D = 2048
KC = 16
PROJ = 5136
QKV = 3072
DFF = 5632
FCH = DFF // 128
EPS = 1e-6
TP = 1024
NSM = 16
TS_ = 8


class Buf:
    def __init__(self, t, name="", trk=None, excl=False):
        self.t = t
        self.name = name
        self.w = None
        self.r = []
        self.trk = trk or self
        self.excl = excl
        self.pend = False

    def __getitem__(self, k):
        return self.t[k]


class Prog:
    SEM_LIMIT = 30000

    def __init__(self, nc, sems):
        self.nc = nc
        self.free = list(sems)
        self.ops = {k: [] for k in ("pe", "act", "dve", "pool", "sp")}
        self.cnt = {}
        self.sem = {}
        for k in ("pe", "act", "dve", "pool"):
            self.sem[k] = self.free.pop()
            self.cnt[k] = 0
        self.waited = {k: {} for k in self.ops}
        self.dpool = {"sp": [[self.free.pop(), 0] for _ in range(16)],
                      "pool": [[self.free.pop(), 0] for _ in range(12)]}
        self.dnext = {"sp": 0, "pool": 0}
        self.out_tokens = []
        self.ninstr = 0

    def _wait(self, eng, tok):
        sem, val = tok
        if self.waited[eng].get(sem, 0) >= val:
            return
        self.waited[eng][sem] = val
        self.ops[eng].append(("w", sem, val))

    def _deps(self, eng, r, w):
        toks = []
        for b in r:
            if b.w is not None:
                toks.append(b.w)
        for b in w:
            if b.w is not None:
                toks.append(b.w)
            toks.extend(b.r)
        for tok in toks:
            if eng == "pe" and tok[0] is self.sem["pe"]:
                continue
            self._wait(eng, tok)

    def _roll(self, eng):
        if self.cnt[eng] >= self.SEM_LIMIT:
            self.sem[eng] = self.free.pop()
            self.cnt[eng] = 0

    @staticmethod
    def _norm(r, w):
        r2 = []
        w2 = [b.trk for b in w]
        for b in r:
            b = b.trk
            if b.excl:
                if b not in w2:
                    w2.append(b)
            else:
                r2.append(b)
        return r2, w2

    def E(self, eng, fn, r=(), w=(), inc=True):
        if DEAD[0]:
            return None
        for b in r:
            b.pend = False
        for b in w:
            b.pend = True
        r, w = self._norm(r, w)
        self._deps(eng, r, w)
        self.ninstr += 1
        self._roll(eng)
        if inc:
            self.cnt[eng] += 1
            tok = (self.sem[eng], self.cnt[eng])
            self.ops[eng].append(("i", fn, self.sem[eng]))
        else:
            tok = (self.sem[eng], self.cnt[eng] + 1)
            self.ops[eng].append(("i", fn, None))
        for b in r:
            b.r.append(tok)
            if len(b.r) > 64:
                b.r = b.r[-48:]
        for b in w:
            b.w = tok
            b.r = []
        return tok

    def D(self, q, out, in_, r=(), w=(), is_output=False, **kw):
        if DEAD[0]:
            return None
        r, w = self._norm(r, w)
        self._deps(q, r, w)
        self.ninstr += 1
        pool = self.dpool[q]
        i = self.dnext[q]
        self.dnext[q] = (i + 1) % len(pool)
        ent = pool[i]
        if ent[1] > 0:
            self._wait(q, (ent[0], ent[1]))
        ent[1] += 16
        tok = (ent[0], ent[1])
        self.ops[q].append(("d", out, in_, ent[0], kw))
        for b in r:
            b.r.append(tok)
        for b in w:
            b.w = tok
            b.r = []
        if is_output:
            self.out_tokens.append(tok)
        return tok

    def fence(self):
        if DEAD[0]:
            return
        toks = []
        for k in ("pe", "act", "dve", "pool"):
            if self.cnt[k] > 0:
                toks.append((self.sem[k], self.cnt[k]))
        for q in ("sp", "pool"):
            for ent in self.dpool[q]:
                if ent[1] > 0:
                    toks.append((ent[0], ent[1]))
        for eng in self.ops:
            for tok in toks:
                if eng in ("pe", "act", "dve", "pool") and tok[0] is self.sem[eng]:
                    continue
                self._wait(eng, tok)

    def finish(self):
        self.fence()

    def emit(self, block):
        ops = self.ops

        def run(e, lst):
            for op in lst:
                if op[0] == "w":
                    e.wait_ge(op[1], op[2])
                elif op[0] == "i":
                    ins = op[1](e)
                    if op[2] is not None:
                        ins.then_inc(op[2], 1)
                else:
                    e.dma_start(out=op[1], in_=op[2], **op[4]).then_inc(op[3], 16)

        @block.tensor
        def _(e):
            run(e, ops["pe"])

        @block.scalar
        def _(e):
            run(e, ops["act"])

        @block.vector
        def _(e):
            run(e, ops["dve"])

        @block.gpsimd
        def _(e):
            run(e, ops["pool"])

        @block.sync
        def _(e):
            run(e, ops["sp"])


STOP = [99]


class _Stop(Exception):
    pass


DEAD = [False]


def ck(k):
    if STOP[0] == k:
        DEAD[0] = True


class Seg:
    def __init__(self, name, T, nseq):
        self.name = name
        self.T = T
        self.nseq = nseq
        self.ncol = T * nseq


def build_program():
    nc = bass.Bass("TRN2", target_bir_lowering=False)
    di = {}

    def din(name, shape):
        di[name] = nc.dram_tensor(name, list(shape), F32, kind="ExternalInput").ap()
        return di[name]

    def dout(name, shape):
        di[name] = nc.dram_tensor(name, list(shape), F32, kind="ExternalOutput").ap()
        return di[name]

    xp = din("xp", [TP, D]); xs = din("xs", [128, D]); xpre = din("xpre", [TP, D]); flag = din("flag", [128, 1])
    call = din("call", [17, D])
    sconv = din("sconv", [48, QKV]); sdelta = din("sdelta", [NSM, 8, 128, 128])
    sre = din("sre", [NSM, 4096]); sim = din("sim", [NSM, 4096])
    w_ada = din("w_ada", [D, 6 * D]); b_ada = din("b_ada", [96, 128])
    gains = din("gains", [64, 128])
    w_in = din("w_in", [D, PROJ]); w_conv = din("w_conv", [96, 128])
    hv = din("hv", [8, 2])
    gdn = din("gdn", [1, 128])
    lam_re = din("lam_re", [32, 128]); lam_im = din("lam_im", [32, 128]); lstep = din("lstep", [32, 2])
    b_re = din("b_re", [4096, 16]); b_im = din("b_im", [4096, 16])
    c_re = din("c_re", [1024, 64]); c_im = din("c_im", [1024, 64])
    dskip = din("dskip", [8, 128]); gs5 = din("gs5", [8, 128])
    w_glu = din("w_glu", [1024, 2048]); w_out = din("w_out", [D, D])
    w_gate = din("w_gate", [D, DFF]); w_up = din("w_up", [D, DFF]); w_down = din("w_down", [DFF, D])
    k_id = din("k_id", [128, 128]); k_mneg = din("k_mneg", [128, 128]); k_strict = din("k_strict", [128, 128])
    k_sel = din("k_sel", [8, 1024]); k_cmask = din("k_cmask", [128, 512]); k_bmask = din("k_bmask", [128, 128])

    yp = dout("yp", [TP, D]); ys = dout("ys", [128, D])
    convp = dout("convp", [3, QKV]); deltap = dout("deltap", [8, 128, 128])
    srep = dout("srep", [32, 128]); simp = dout("simp", [32, 128])
    convs = dout("convs", [48, QKV]); deltas = dout("deltas", [NSM, 8, 128, 128])
    sres = dout("sres", [NSM, 4096]); sims = dout("sims", [NSM, 4096])
    x1d = Buf(nc.dram_tensor("x1d", [TP + 128, D], F32).ap(), "x1d")
    fd = Buf(nc.dram_tensor("fd", [TP + 128, D], F32).ap(), "fd")

    PRs = Seg("pr", TP, 1)
    PREs = Seg("pre", TP, 1)
    SMs = Seg("sm", TS_, NSM)

    with contextlib.ExitStack() as st:
        uid = [0]

        def sb(name, shape, dt=F32, stack=None):
            uid[0] += 1
            return Buf((stack or st).enter_context(nc.sbuf_tensor(f"{name}_{uid[0]}", list(shape), dt)), name)

        def ps(name, shape, dt=F32):
            return Buf(st.enter_context(nc.psum_tensor(name, list(shape), dt)), name, excl=True)

        sems = [st.enter_context(nc.semaphore(f"s{i}")) for i in range(96)]
        P = Prog(nc, sems)

        def MM(out, lhsT, rhs, r, w, start=True, stop=True, inc=True):
            P.E("pe", lambda e: e.matmul(out, lhsT, rhs, start=start, stop=stop), r, w, inc)

        def TR(out, in_, ident, r, w):
            P.E("pe", lambda e: e.transpose(out, in_, ident), r, w)

        def ACT(out, in_, func, r, w, scale=1.0):
            P.E("act", lambda e: e.activation(out=out, in_=in_, func=func, scale=scale), r, w)

        def TSC(out, in0, s1, s2, op0, op1, r, w):
            if s2 is None:
                P.E("dve", lambda e: e.tensor_scalar(out=out, in0=in0, scalar1=s1, scalar2=None, op0=op0), r, w)
            else:
                P.E("dve", lambda e: e.tensor_scalar(out=out, in0=in0, scalar1=s1, scalar2=s2, op0=op0, op1=op1), r, w)

        def TT(out, in0, in1, op, r, w):
            P.E("dve", lambda e: e.tensor_tensor(out=out, in0=in0, in1=in1, op=op), r, w)

        def TTP(out, in0, in1, op, r, w):
            P.E("pool", lambda e: e.tensor_tensor(out=out, in0=in0, in1=in1, op=op), r, w)

        def STT(out, in0, scalar, in1, op0, op1, r, w):
            P.E("dve", lambda e: e.scalar_tensor_tensor(out=out, in0=in0, scalar=scalar, in1=in1, op0=op0, op1=op1), r, w)

        def CP(eng, out, in_, r, w):
            if eng == "act":
                P.E("act", lambda e: e.activation(out=out, in_=in_, func=AF.Copy), r, w)
            else:
                P.E("dve", lambda e: e.tensor_copy(out=out, in_=in_), r, w)

        def MEMSET(ap, val, w):
            P.E("dve", lambda e: e.memset(ap, val), (), w)

        def RSUM(out, in_, r, w):
            P.E("dve", lambda e: e.reduce_sum(out=out, in_=in_, axis=AX.X), r, w)

        def RECIP(out, in_, r, w):
            P.E("dve", lambda e: e.reciprocal(out=out, in_=in_), r, w)

        def SCAN(out, d0, d1, init, r, w):
            P.E("dve", lambda e: e.tensor_tensor_scan(out=out, data0=d0, data1=d1, initial=init,
                                                      op0=ALU.mult, op1=ALU.add), r, w)

        pbig = [ps(f"pb{i}", [128, 512]) for i in range(4)]
        ptr_t = ps("ptr", [128, 1024], BF16)
        ptr = [Buf(ptr_t.t[:, i * 512:(i + 1) * 512], f"ptr{i}", trk=ptr_t) for i in range(2)]
        psm_t = [ps(f"psmt{i}", [128, 512]) for i in range(3)]
        psm = [Buf(psm_t[i % 3].t[:, (i // 3) * 128:(i // 3) * 128 + 128], f"psm{i}", trk=psm_t[i % 3]) for i in range(12)]
        ctr = {"big": 0, "tr": 0, "sm": 0, "w": 0, "cp": 0}

        def big():
            ctr["big"] += 1
            t_ = pbig[ctr["big"] % 3]
            assert not t_.pend, f"psum big tile {t_.name} reused while unread"
            return t_

        def ptrn():
            ctr["tr"] += 1
            return ptr[ctr["tr"] % 2]

        psm_ext = []
        for q_ in range(4):
            for b_ in range(7):
                if b_ < 3:
                    psm_ext.append(psm[q_ * 3 + b_])
                else:
                    psm_ext.append(Buf(pbig[b_ - 3].t[:, q_ * 128:(q_ + 1) * 128], f"psx{b_}{q_}", trk=pbig[b_ - 3]))
        use_ext = [False]

        def small():
            ctr["sm"] += 1
            if use_ext[0]:
                t_ = psm_ext[ctr["sm"] % 28]
                assert not t_.pend, f"psum ext tile {t_.name} reused while unread"
                return t_
            t_ = psm[ctr["sm"] % 12]
            assert not t_.pend, f"psum small tile {t_.name} reused while unread"
            return t_

        def cpeng():
            ctr["cp"] += 1
            return "act" if ctr["cp"] % 2 else "dve"

        id32 = sb("id32", [128, 128]); idb = sb("idb", [128, 128], BF16)
        mneg = sb("mneg", [128, 128]); strict = sb("strict", [128, 128])
        sel = sb("sel", [8, 1024]); bmask = sb("bmask", [128, 128])
        ones = sb("ones", [128, 128]); onesb = sb("onesb", [128, 128], BF16)
        P.D("sp", id32[:], k_id, w=[id32]); P.D("pool", idb[:], k_id, w=[idb])
        P.D("sp", mneg[:], k_mneg, w=[mneg]); P.D("sp", strict[:], k_strict, w=[strict])
        P.D("sp", sel[:], k_sel, w=[sel]); P.D("sp", bmask[:], k_bmask, w=[bmask])
        MEMSET(ones[:], 1.0, [ones]); MEMSET(onesb[:], 1.0, [onesb])

        wbufs = [sb(f"wch{i}", [128, 16, 128], BF16) for i in range(2)]

        def load_w(W, r0, nk, c0, ncols):
            ctr["w"] += 1
            b = wbufs[ctr["w"] % 2]
            src = W[r0:r0 + nk * 128, c0:c0 + ncols].rearrange("(k p) n -> p k n", p=128)
            P.D("pool", b.t[:, 0:nk, 0:ncols], src, w=[b])
            return b

        stg = sb("stg", [128, 128])

        def load_fm(dst, src_ap, n):
            P.D("sp", stg.t[0:n, :], src_ap, w=[stg])
            p = small()
            TR(p.t[:, 0:n], stg.t[0:n, :], id32.t[0:n, 0:n], [stg, id32], [p])
            CP("dve", dst, p.t[:, 0:n], [p], [])

        try:
            bada = sb("bada", [128, 96]); load_fm(bada[:], b_ada, 96); bada.w = None
            gn = sb("gn", [128, 64]); load_fm(gn[:], gains, 64)
            wcv = sb("wcv", [128, 96]); load_fm(wcv[:], w_conv, 96)
            dsk = sb("dsk", [128, 8]); load_fm(dsk[:], dskip, 8)
            g5 = sb("g5", [128, 8]); load_fm(g5[:], gs5, 8)
            gdc = sb("gdc", [128, 1]); load_fm(gdc[:], gdn, 1)
            hvb = sb("hvb", [8, 2]); P.D("sp", hvb[:], hv, w=[hvb])
            nA = sb("nA", [8, 1])
            ACT(nA[:], hvb.t[:, 0:1], AF.Exp, [hvb], [nA])
            TSC(nA[:], nA[:], -1.0, None, ALU.mult, None, [nA], [nA])
            P.fence()
            ck(1)

            mod = sb("mod", [128, 96, 17])

            ck(2)
            gtr1 = sb("gtrow", [128, D]); gtrow = {"pr": gtr1, "sm": gtr1}
            xrep = sb("xrep", [128, 128])

            def build_gtrow(base, seg):
                for kc in range(KC):
                    if seg.nseq == 1:
                        TSC(xrep[:], ones[:], mod.t[:, base + kc, 0:1], None, ALU.mult, None, [ones, mod], [xrep])
                        p = small()
                        MM(p[:], xrep[:], id32[:], [xrep, id32], [p])
                        CP("act", gtrow["pr"].t[:, kc * 128:(kc + 1) * 128], p[:], [p], [gtrow["pr"]])
                        continue
                    for t in range(TS_):
                        CP("dve", xrep.t[:, t * 16:(t + 1) * 16], mod.t[:, base + kc, 1:17], [mod], [xrep])
                    p = small()
                    MM(p[:], xrep[:], id32[:], [xrep, id32], [p])
                    CP("act", gtrow["sm"].t[:, kc * 128:(kc + 1) * 128], p[:], [p], [gtrow["sm"]])

            xt = sb("xt", [128, D]); sqb = sb("sqb", [128, D]); xn = sb("xn", [128, D], BF16)
            cols4 = [sb(f"col{i}", [128, 1]) for i in range(4)]
            hfm = sb("hfm", [128, 16, 512], BF16)
            hfm_g = hfm
            resb = sb("resb", [128, D])

            def rstd_of(src_ap, srcbufs, n, scale):
                ACT(sqb.t[:, 0:n], src_ap, AF.Square, srcbufs, [sqb])
                ck(41)
                RSUM(cols4[0][:], sqb.t[:, 0:n], [sqb], [cols4[0]])
                ck(42)
                TSC(cols4[1][:], cols4[0][:], scale, EPS, ALU.mult, ALU.add, [cols4[0]], [cols4[1]])
                ACT(cols4[2][:], cols4[1][:], AF.Sqrt, [cols4[1]], [cols4[2]])
                ck(43)
                RECIP(cols4[3][:], cols4[2][:], [cols4[2]], [cols4[3]])
                ck(44)
                return cols4[3]

            def make_h(seg, src_ap, srcbufs, scg, sh, col0, hdst=None):
                hfm = hdst or hfm_g
                P.D("sp", xt[:], src_ap, r=srcbufs, w=[xt])
                rs = rstd_of(xt[:], [xt], D, 1.0 / D)
                TSC(xn[:], xt[:], rs.t[:, 0:1], None, ALU.mult, None, [xt, rs], [xn])
                ck(45)
                for kc in range(KC):
                    p = ptrn()
                    TR(p.t[:, 0:128], xn.t[:, kc * 128:(kc + 1) * 128], idb[:], [xn, idb], [p])
                    if seg.nseq == 1:
                        TSC(hfm.t[:, kc, col0:col0 + 128], p.t[:, 0:128], mod.t[:, scg + kc, 0:1], mod.t[:, sh + kc, 0:1],
                            ALU.mult, ALU.add, [p, mod], [hfm])
                    else:
                        for s in range(NSM):
                            TSC(hfm.t[:, kc, col0 + s:col0 + 128:16], p.t[:, s:128:16], mod.t[:, scg + kc, 1 + s:2 + s],
                                mod.t[:, sh + kc, 1 + s:2 + s], ALU.mult, ALU.add, [p, mod], [hfm])

            def proj(W, col0, ncols, nb, nk=KC, src=None):
                src = src or hfm
                wb = load_w(W, 0, nk, col0, ncols)
                p = big()
                for kc in range(nk):
                    MM(p.t[0:ncols, 0:nb], wb.t[:, kc, 0:ncols], src.t[:, kc, 0:nb], [wb, src], [p],
                       start=(kc == 0), stop=(kc == nk - 1), inc=(kc == nk - 1))
                return p

            msc = contextlib.ExitStack()
            st.enter_context(msc)
            Wb = [sb("wb_re", [128, 32, 128], BF16, msc), sb("wb_im", [128, 32, 128], BF16, msc)]
            Wc = [sb("wc_re", [128, 32, 128], BF16, msc), sb("wc_im", [128, 32, 128], BF16, msc)]
            tabc = sb("tabc", [128, 32, 128], F32, msc); tabs = sb("tabs", [128, 32, 128], F32, msc)
            t128 = [sb("t128c", [128, 32], F32, msc), sb("t128s", [128, 32], F32, msc)]
            abre = sb("abre", [128, 32], F32, msc); abim = sb("abim", [128, 32], F32, msc); nabim = sb("nabim", [128, 32], F32, msc); magc = sb("magc", [128, 32], F32, msc)
            with contextlib.ExitStack() as sc:
                lre = sb("lre", [128, 32], F32, sc); lim = sb("lim", [128, 32], F32, sc); dts = sb("dts", [128, 32], F32, sc)
                ls2 = sb("ls2", [32, 2], F32, sc); lsx = sb("lsx", [32, 128], F32, sc)
                tA = [sb(f"s5t{i}", [128, 32], F32, sc) for i in range(8)]
                uc = sb("uc", [128, 32], F32, sc); us = sb("us", [128, 32], F32, sc)
                core = sb("core", [128, 32], F32, sc); coim = sb("coim", [128, 32], F32, sc)
                brt = sb("brt", [128, 32, 16], F32, sc); bit = sb("bit", [128, 32, 16], F32, sc)
                crt = sb("crt", [128, 8, 64], F32, sc); cit = sb("cit", [128, 8, 64], F32, sc)
                Z = sb("Z", [128, 128], F32, sc); t16 = sb("t16", [128, 16], F32, sc)
                hp = sb("hp", [128, 1], F32, sc)
                cmask = sb("cmask", [128, 512], F32, sc)
                P.D("sp", cmask[:], k_cmask, w=[cmask])
                load_fm(lre[:], lam_re, 32); lre.w = None
                load_fm(lim[:], lam_im, 32); lim.w = None
                P.D("sp", ls2[:], lstep, w=[ls2])
                TSC(lsx.t[:, 0:64], ones.t[0:32, 0:64], ls2.t[:, 0:1], None, ALU.mult, None, [ones, ls2], [lsx])
                TSC(lsx.t[:, 64:128], ones.t[0:32, 0:64], ls2.t[:, 1:2], None, ALU.mult, None, [ones, ls2], [lsx])
                p = small()
                TR(p.t[:, 0:32], lsx[:], id32.t[0:32, 0:32], [lsx, id32], [p])
                ACT(dts[:], p.t[:, 0:32], AF.Exp, [p], [dts])
                P.D("sp", brt[:], b_re.rearrange("(t p) h -> p t h", p=128), w=[brt])
                P.D("sp", bit[:], b_im.rearrange("(t p) h -> p t h", p=128), w=[bit])
                P.D("sp", crt[:], c_re.rearrange("(t p) h -> p t h", p=128), w=[crt])
                P.D("sp", cit[:], c_im.rearrange("(t p) h -> p t h", p=128), w=[cit])
                P.fence()
                TSC(lre[:], lre[:], -1e-4, None, ALU.min, None, [lre], [lre])
                TT(tA[0][:], lre[:], dts[:], ALU.mult, [lre, dts], [tA[0]])
                ACT(magc[:], tA[0][:], AF.Exp, [tA[0]], [magc])
                TT(tA[1][:], lim[:], dts[:], ALU.mult, [lim, dts], [tA[1]])
                MEMSET(hp[:], math.pi / 2, [hp])
                P.E("act", lambda e: e.activation(out=us[:], in_=tA[1][:], func=AF.Sin, scale=1.0 / 32), [tA[1]], [us])
                P.E("act", lambda e: e.activation(out=uc[:], in_=tA[1][:], func=AF.Sin, scale=1.0 / 32, bias=hp[:]), [tA[1], hp], [uc])

                def csquare(c, s):
                    TT(tA[2][:], c[:], c[:], ALU.mult, [c], [tA[2]])
                    TT(tA[3][:], s[:], s[:], ALU.mult, [s], [tA[3]])
                    TT(tA[4][:], c[:], s[:], ALU.mult, [c, s], [tA[4]])
                    TT(c[:], tA[2][:], tA[3][:], ALU.subtract, [tA[2], tA[3]], [c])
                    TSC(s[:], tA[4][:], 2.0, None, ALU.mult, None, [tA[4]], [s])

                for _ in range(5):
                    csquare(uc, us)
                TT(abre[:], magc[:], uc[:], ALU.mult, [magc, uc], [abre])
                TT(abim[:], magc[:], us[:], ALU.mult, [magc, us], [abim])
                TSC(nabim[:], abim[:], -1.0, None, ALU.mult, None, [abim], [nabim])
                TSC(tA[0][:], abre[:], -1.0, None, ALU.add, None, [abre], [tA[0]])
                TT(tA[1][:], lre[:], lre[:], ALU.mult, [lre], [tA[1]])
                TT(tA[2][:], lim[:], lim[:], ALU.mult, [lim], [tA[2]])
                TT(tA[1][:], tA[1][:], tA[2][:], ALU.add, [tA[1], tA[2]], [tA[1]])
                RECIP(tA[5][:], tA[1][:], [tA[1]], [tA[5]])
                TT(tA[2][:], tA[0][:], lre[:], ALU.mult, [tA[0], lre], [tA[2]])
                TT(tA[3][:], abim[:], lim[:], ALU.mult, [abim, lim], [tA[3]])
                TT(tA[2][:], tA[2][:], tA[3][:], ALU.add, [tA[2], tA[3]], [tA[2]])
                TT(core[:], tA[2][:], tA[5][:], ALU.mult, [tA[2], tA[5]], [core])
                TT(tA[2][:], abim[:], lre[:], ALU.mult, [abim, lre], [tA[2]])
                TT(tA[3][:], tA[0][:], lim[:], ALU.mult, [tA[0], lim], [tA[3]])
                TT(tA[2][:], tA[2][:], tA[3][:], ALU.subtract, [tA[2], tA[3]], [tA[2]])
                TT(coim[:], tA[2][:], tA[5][:], ALU.mult, [tA[2], tA[5]], [coim])
                pc = sb("pc", [128, 32], F32, sc); pss = sb("pss", [128, 32], F32, sc)
                CP("dve", pc[:], uc[:], [uc], [pc]); CP("dve", pss[:], us[:], [us], [pss])
                MEMSET(tabc.t[:, :, 0:1], 1.0, [tabc]); MEMSET(tabs.t[:, :, 0:1], 0.0, [tabs])
                tmpT = sb("tmpT", [128, 32, 64], F32, sc)
                n = 1
                while n < 128:
                    pcb = pc.t[:, :].unsqueeze(2).broadcast_to([128, 32, n])
                    psb = pss.t[:, :].unsqueeze(2).broadcast_to([128, 32, n])
                    tmp = tmpT.t[:, :, 0:n]
                    TT(tmp, tabs.t[:, :, 0:n], psb, ALU.mult, [tabs, pss], [tmpT])
                    TT(tabc.t[:, :, n:2 * n], tabc.t[:, :, 0:n], pcb, ALU.mult, [tabc, pc], [tabc])
                    TT(tabc.t[:, :, n:2 * n], tabc.t[:, :, n:2 * n], tmp, ALU.subtract, [tabc, tmpT], [tabc])
                    TT(tmp, tabs.t[:, :, 0:n], pcb, ALU.mult, [tabs, pc], [tmpT])
                    TT(tabs.t[:, :, n:2 * n], tabc.t[:, :, 0:n], psb, ALU.mult, [tabc, pss], [tabs])
                    TT(tabs.t[:, :, n:2 * n], tabs.t[:, :, n:2 * n], tmp, ALU.add, [tabs, tmpT], [tabs])
                    csquare(pc, pss)
                    n *= 2
                CP("dve", t128[0][:], pc[:], [pc], [t128[0]]); CP("dve", t128[1][:], pss[:], [pss], [t128[1]])
                def gen_w():
                    for ti in range(32):
                        q = ti % 4
                        for part, (ca, cb_, sgn) in enumerate(((core, coim, ALU.subtract), (core, coim, ALU.add))):
                            MEMSET(Z[:], 0.0, [Z])
                            for g2 in range(2):
                                rows = slice(g2 * 64, g2 * 64 + 64)
                                cs_ = slice((2 * q + g2) * 16, (2 * q + g2) * 16 + 16)
                                if part == 0:
                                    TSC(t16.t[rows, :], bit.t[rows, ti, :], coim.t[rows, ti:ti + 1], None, ALU.mult, None, [bit, coim], [t16])
                                    STT(Z.t[rows, cs_], brt.t[rows, ti, :], core.t[rows, ti:ti + 1], t16.t[rows, :], ALU.mult, ALU.subtract, [brt, core, t16], [Z])
                                else:
                                    TSC(t16.t[rows, :], brt.t[rows, ti, :], coim.t[rows, ti:ti + 1], None, ALU.mult, None, [brt, coim], [t16])
                                    STT(Z.t[rows, cs_], bit.t[rows, ti, :], core.t[rows, ti:ti + 1], t16.t[rows, :], ALU.mult, ALU.add, [bit, core, t16], [Z])
                            p = small()
                            TR(p[:], Z[:], id32[:], [Z, id32], [p])
                            CP("act", Wb[part].t[:, ti, :], p[:], [p], [Wb[part]])
                            yield
                    for ti in range(32):
                        q = ti % 4; fcx = ti // 4
                        for part, cc in enumerate((crt, cit)):
                            for g2 in range(2):
                                TT(Z.t[:, g2 * 64:(g2 + 1) * 64], cc.t[:, fcx, :], cmask.t[:, q * 128 + g2 * 64:q * 128 + (g2 + 1) * 64],
                                   ALU.mult, [cc, cmask], [Z])
                            p = small()
                            TR(p[:], Z[:], id32[:], [Z, id32], [p])
                            if part == 0:
                                CP("act", Wc[0].t[:, ti, :], p[:], [p], [Wc[0]])
                            else:
                                TSC(Wc[1].t[:, ti, :], p[:], -1.0, None, ALU.mult, None, [p], [Wc[1]])
                            yield

                with contextlib.ExitStack() as sc:
                    craw = sb("craw", [17, D], F32, sc); csl = sb("csl", [17, D], BF16, sc)
                    cT = sb("cT", [128, 16, 17], BF16, sc)
                    P.D("sp", craw[:], call, w=[craw])
                    ACT(csl[:], craw[:], AF.Silu, [craw], [csl])
                    for kc in range(KC):
                        p = ptrn()
                        TR(p.t[:, 0:17], csl.t[0:17, kc * 128:(kc + 1) * 128], idb.t[0:17, 0:17], [csl, idb], [p])
                        CP("dve", cT.t[:, kc, :], p.t[:, 0:17], [p], [cT])
                    gw = gen_w()
                    for f in range(96):
                        next(gw, None); next(gw, None)
                        wb = load_w(w_ada, 0, 16, f * 128, 128)
                        p = small()
                        for kc in range(KC):
                            MM(p.t[:, 0:17], wb.t[:, kc, :], cT.t[:, kc, :], [wb, cT], [p],
                               start=(kc == 0), stop=(kc == KC - 1), inc=(kc == KC - 1))
                        TSC(mod.t[:, f, :], p.t[:, 0:17], bada.t[:, f:f + 1], None, ALU.add, None, [p, bada], [mod])
                    for kc in range(KC):
                        TSC(mod.t[:, 16 + kc, :], mod.t[:, 16 + kc, :], 1.0, gn.t[:, kc:kc + 1], ALU.add, ALU.mult, [mod, gn], [mod])
                        TSC(mod.t[:, 32 + kc, :], mod.t[:, 32 + kc, :], gn.t[:, 16 + kc:17 + kc], None, ALU.mult, None, [mod, gn], [mod])
                        TSC(mod.t[:, 64 + kc, :], mod.t[:, 64 + kc, :], 1.0, gn.t[:, 32 + kc:33 + kc], ALU.add, ALU.mult, [mod, gn], [mod])
                        TSC(mod.t[:, 80 + kc, :], mod.t[:, 80 + kc, :], gn.t[:, 48 + kc:49 + kc], None, ALU.mult, None, [mod, gn], [mod])
                P.fence()
                for _ in gw:
                    pass
            P.fence()

            ck(3)
            mixin = sb("mixin", [128, 16, 512], BF16, msc)
            hist = sb("hist", [128, 24, 48], F32, msc)
            Sst = [sb(f"Sst{h}", [128, 128], F32, msc) for h in range(8)]
            gin = [sb("gin_re", [128, 32], F32, msc), sb("gin_im", [128, 32], F32, msc)]
            hfin = [sb("hfin_re", [128, 32], F32, msc), sb("hfin_im", [128, 32], F32, msc)]
            stiny = stg

            def mixer(seg, xsrc, ydst_off, so=False, first=True):
                nseq = seg.nseq
                hc = 3 * nseq
                nblk = (seg.ncol + 511) // 512
                c = 128 if nseq == 1 else TS_
                nlev = int(math.log2(c)) - 1
                build_state = (nseq > 1)
                for blk in range(nblk):
                    c0 = blk * 512
                    nb = min(512, seg.ncol - c0)
                    ntile = nb // 128
                    for tl in range(ntile):
                        make_h(seg, xsrc[c0 + tl * 128:c0 + (tl + 1) * 128, :], [], 16, 0, tl * 128)
                    ck(5)
                    with contextlib.ExitStack() as sc:
                        pre = [sb(f"pre{i}", [128, 48 + 512], F32, sc) for i in range(3)]
                        post = [sb(f"post{i}", [128, 512], F32, sc) for i in range(3)]
                        sz = sb("sz", [128, 512], F32, sc)
                        ctmp = sb("ctmp", [128, 512], F32, sc)
                        gf = sb("gf", [8, 512], F32, sc); gcum = sb("gcum", [8, 512], F32, sc)
                        beta = sb("beta", [8, 512], F32, sc); nbeta = sb("nbeta", [8, 512], F32, sc)
                        egc = sb("egc", [8, 512], F32, sc); gtmp = sb("gtmp", [8, 512], F32, sc); ekd = gtmp
                        nun = ntile if nseq == 1 else NSM
                        tcol = sb("tcol", [128, nun, 40], F32, sc)
                        dl = {k: sb("dl_" + k, [128, 128], F32, sc) for k in
                              ("dt", "dec", "dS", "Nm", "NmT", "Y0", "Y1", "P0", "P1", "T0", "T1", "ATm", "kd", "vtok",
                               "Rn", "vnew", "otok")}
                        dl["sq"] = dl["dt"]; dl["on"] = dl["dec"]; dl["P3"] = dl["dS"]
                        NSLOT = 5 if nseq > 1 else 4
                        sbf_t = sb("sbf", [128, 5, 128], BF16, sc)
                        sbf = [Buf(sbf_t.t[:, i, :], f"sbf{i}") for i in range(5)]
                        scols = sb("scols", [128, 25], F32, sc)
                        names17 = ("dt", "dec", "dS", "Nm", "NmT", "Y0", "Y1", "P0", "P1", "T0", "T1", "ATm", "kd", "vtok", "Rn", "vnew", "otok")
                        slots = []
                        for sl in range(NSLOT):
                            if sl == 0:
                                d_ = dl
                            elif sl == 4:
                                xnf = xn.t[:, :].bitcast(F32)
                                regs = [gtr1.t[:, i * 128:(i + 1) * 128] for i in range(16)] + [xnf[:, 7 * 128:8 * 128]]
                                d_ = {nm: Buf(regs[i], f"v{sl}{nm}") for i, nm in enumerate(names17)}
                            elif sl == 3:
                                xnf = xn.t[:, :].bitcast(F32)
                                regs = [resb.t[:, (6 + i) * 128:(7 + i) * 128] for i in range(10)] + [xnf[:, i * 128:(i + 1) * 128] for i in range(7)]
                                d_ = {nm: Buf(regs[i], f"v{sl}{nm}") for i, nm in enumerate(names17)}
                            else:
                                base_ = sqb if sl == 1 else xt
                                d_ = {nm: Buf(base_.t[:, i * 128:(i + 1) * 128], f"v{sl}{nm}") for i, nm in enumerate(names17[:16])}
                                d_["otok"] = Buf(resb.t[:, (sl - 1) * 128:sl * 128], f"v{sl}otok")
                            if sl > 0:
                                d_["sq"] = d_["dt"]; d_["on"] = d_["dec"]; d_["P3"] = d_["dS"]
                            eg_ = Buf(scols.t[:, 5 * sl:5 * sl + 1], f"egl{sl}")
                            dc_ = [Buf(scols.t[:, 5 * sl + 1 + i:5 * sl + 2 + i], f"dc{sl}{i}") for i in range(4)]
                            slots.append((d_, eg_, dc_))
                        Ssm = [Buf(resb.t[:, (2 + i) * 128:(3 + i) * 128], f"Ssm{i}") for i in range(4)] + [stg]
                        P.fence()
                        cst = sb("cst", [48, 128], F32, sc)
                        if blk == 0:
                            if nseq == 1 and first:
                                MEMSET(hist[:], 0.0, [hist])
                                for h in range(8):
                                    MEMSET(Sst[h][:], 0.0, [Sst[h]])
                            elif nseq == 1:
                                for ch in range(24):
                                    TSC(hist.t[:, ch, 0:3], hist.t[:, ch, 0:3], flg.t[:, 0:1], None, ALU.mult, None, [hist, flg], [hist])
                                for h in range(8):
                                    TSC(Sst[h][:], Sst[h][:], flg.t[:, 0:1], None, ALU.mult, None, [Sst[h], flg], [Sst[h]])
                            else:
                                for ch in range(24):
                                    P.D("sp", cst.t[0:48, :], sconv[0:48, ch * 128:(ch + 1) * 128], w=[cst])
                                    p = small()
                                    TR(p.t[:, 0:48], cst.t[0:48, :], id32.t[0:48, 0:48], [cst, id32], [p])
                                    CP("dve", hist.t[:, ch, 0:48], p.t[:, 0:48], [p], [hist])
                        ck(51)
                        p = proj(w_in, 4096, 8, nb)
                        TSC(gtmp.t[:, 0:nb], p.t[0:8, 0:nb], hvb.t[:, 1:2], None, ALU.add, None, [p, hvb], [gtmp])
                        ck(52)
                        ACT(gtmp.t[:, 0:nb], gtmp.t[:, 0:nb], AF.Exp, [gtmp], [gtmp])
                        TSC(gtmp.t[:, 0:nb], gtmp.t[:, 0:nb], 1.0, None, ALU.add, None, [gtmp], [gtmp])
                        ACT(gf.t[:, 0:nb], gtmp.t[:, 0:nb], AF.Ln, [gtmp], [gf])
                        TSC(gf.t[:, 0:nb], gf.t[:, 0:nb], nA.t[:, 0:1], None, ALU.mult, None, [gf, nA], [gf])
                        ck(53)
                        p = proj(w_in, 4104, 8, nb)
                        ACT(beta.t[:, 0:nb], p.t[0:8, 0:nb], AF.Sigmoid, [p], [beta])
                        TSC(nbeta.t[:, 0:nb], beta.t[:, 0:nb], -1.0, None, ALU.mult, None, [beta], [nbeta])
                        ck(54)
                        if nseq == 1:
                            for ci in range(ntile):
                                cs = slice(ci * 128, ci * 128 + 128)
                                SCAN(gcum.t[:, cs], ones.t[0:8, 0:128], gf.t[:, cs], 0.0, [ones, gf], [gcum])
                                TSC(gtmp.t[:, cs], gcum.t[:, cs], -1.0, gcum.t[:, ci * 128 + 127:ci * 128 + 128], ALU.mult, ALU.add, [gcum], [gtmp])
                        else:
                            CP("dve", gcum.t[:, 0:16], gf.t[:, 0:16], [gf], [gcum])
                            for t in range(1, TS_):
                                TT(gcum.t[:, 16 * t:16 * t + 16], gcum.t[:, 16 * t - 16:16 * t], gf.t[:, 16 * t:16 * t + 16], ALU.add, [gcum, gf], [gcum])
                            for t in range(TS_):
                                TT(gtmp.t[:, 16 * t:16 * t + 16], gcum.t[:, 112:128], gcum.t[:, 16 * t:16 * t + 16], ALU.subtract, [gcum], [gtmp])
                        ck(55)
                        ACT(gtmp.t[:, 0:nb], gtmp.t[:, 0:nb], AF.Exp, [gtmp], [gtmp])
                        ACT(egc.t[:, 0:nb], gcum.t[:, 0:nb], AF.Exp, [gcum], [egc])
                        ck(56)
                        for un in range(nun):
                            cs = slice(un * 128, un * 128 + 128) if nseq == 1 else slice(un, 128, 16)
                            p = small()
                            for k5, X in enumerate((gcum, beta, nbeta, egc, ekd)):
                                TR(p.t[0:c, 8 * k5:8 * k5 + 8], X.t[0:8, cs], id32.t[0:8, 0:8], [X, id32], [p])
                            CP("dve", tcol.t[0:c, un, :], p.t[0:c, 0:40], [p], [tcol])

                        ck(6)
                        for hd in range(8):
                            for wi in range(3):
                                ch = wi * 8 + hd
                                if so and wi == 0 and blk != nblk - 1:
                                    continue
                                p = proj(w_in, wi * 1024 + hd * 128, 128, nb)
                                CP("dve", pre[wi].t[:, 0:hc], hist.t[:, ch, 0:hc], [hist], [pre[wi]])
                                CP("act", pre[wi].t[:, hc:hc + nb], p.t[:, 0:nb], [p], [pre[wi]])
                                CP("dve", hist.t[:, ch, 0:hc], pre[wi].t[:, nb:nb + hc], [pre[wi]], [hist])
                                if so and wi == 0:
                                    continue
                                if blk == nblk - 1 and not so:
                                    pq = small()
                                    TR(pq.t[0:hc, :], pre[wi].t[:, nb:nb + hc], id32[:], [pre[wi], id32], [pq])
                                    CP("dve", cst.t[0:hc, :], pq.t[0:hc, :], [pq], [cst])
                                    dstc = (convp if nseq == 1 else convs)
                                    P.D("sp", dstc[0:hc, ch * 128:(ch + 1) * 128], cst.t[0:hc, :], r=[cst], is_output=True)
                                TSC(ctmp.t[:, 0:nb], pre[wi].t[:, 0:nb], wcv.t[:, ch:ch + 1], None, ALU.mult, None, [pre[wi], wcv], [ctmp])
                                for j in range(1, 4):
                                    STT(ctmp.t[:, 0:nb], pre[wi].t[:, j * nseq:j * nseq + nb], wcv.t[:, j * 24 + ch:j * 24 + ch + 1],
                                        ctmp.t[:, 0:nb], ALU.mult, ALU.add, [pre[wi], wcv, ctmp], [ctmp])
                                ACT(post[wi].t[:, 0:nb], ctmp.t[:, 0:nb], AF.Silu, [ctmp], [post[wi]])
                                if wi < 2:
                                    sqv = ctmp.t[:, :].bitcast(BF16)[:, 0:nb]
                                    TT(sqv, post[wi].t[:, 0:nb], post[wi].t[:, 0:nb], ALU.mult, [post[wi]], [ctmp])
                                    p = big()
                                    MM(p.t[:, 0:nb], onesb[:], sqv, [onesb, ctmp], [p])
                                    TSC(ctmp.t[:, 0:nb], p.t[:, 0:nb], 1e-6, None, ALU.add, None, [p], [ctmp])
                                    ACT(ctmp.t[:, 0:nb], ctmp.t[:, 0:nb], AF.Sqrt, [ctmp], [ctmp])
                                    RECIP(sz.t[:, 0:nb], ctmp.t[:, 0:nb], [ctmp], [sz])
                                    STT(post[wi].t[:, 0:nb], post[wi].t[:, 0:nb], (128.0 ** -0.5) if wi == 0 else 1.0, sz.t[:, 0:nb],
                                        ALU.mult, ALU.mult, [post[wi], sz], [post[wi]])
                            if not so:
                                p = proj(w_in, 3072 + hd * 128, 128, nb)
                                ACT(sz.t[:, 0:nb], p.t[:, 0:nb], AF.Silu, [p], [sz])
                            qn, kn, vv = post[0], post[1], post[2]
                            cbf = ctmp.t[:, :].bitcast(BF16)
                            knb = cbf[:, 0:nb]; qnb = cbf[:, 512:512 + nb]
                            CP("act", knb, kn.t[:, 0:nb], [kn], [ctmp])
                            vb = pre[2].t[:, :].bitcast(BF16)[:, 0:nb]
                            CP("act", vb, vv.t[:, 0:nb], [vv], [pre[2]])
                            if not so:
                                CP("act", qnb, qn.t[:, 0:nb], [qn], [ctmp])
                            ck(7)
                            def unit_gen(un, dl, egl, dcols):
                                cs = slice(un * 128, un * 128 + 128) if nseq == 1 else slice(un, 128, 16)
                                tc = lambda k: tcol.t[0:c, un, k * 8 + hd:k * 8 + hd + 1]
                                if nseq == 1:
                                    S = Sst[hd]
                                    Sb = sbf[0]
                                else:
                                    S = Ssm[un % 5]
                                    Sb = sbf[un % 5]
                                    P.D("sp", S[:], sdelta[un, hd], w=[S])
                                if nseq > 1 or un == 0:
                                    CP("act", Sb[:], S[:], [S], [Sb])
                                bv = lambda nm: dl[nm].t[:, :].bitcast(BF16)[:, 0:128]
                                pA = small(); pB = small(); pC = small()
                                MM(pA.t[:, 0:c], sel.t[0:8, hd * 128:(hd + 1) * 128], gcum.t[0:8, cs], [sel, gcum], [pA])
                                MM(pB.t[0:c, 0:c], knb[:, cs], knb[:, cs], [ctmp], [pB])
                                if not so:
                                    MM(pC.t[0:c, 0:c], knb[:, cs], qnb[:, cs], [ctmp], [pC])
                                else:
                                    pC.pend = False
                                yield
                                STT(dl["dt"].t[0:c, 0:c], pA.t[0:c, 0:c], tc(0), mneg.t[0:c, 0:c], ALU.subtract, ALU.add, [pA, tcol, mneg], [dl["dt"]])
                                ACT(egl[:], pA.t[:, c - 1:c], AF.Exp, [pA], [egl])
                                ACT(dl["dec"].t[0:c, 0:c], dl["dt"].t[0:c, 0:c], AF.Exp, [dl["dt"]], [dl["dec"]])
                                yield
                                if not so:
                                    TT(bv("ATm")[0:c, 0:c], pC.t[0:c, 0:c], dl["dec"].t[0:c, 0:c], ALU.mult, [pC, dl["dec"]], [dl["ATm"]])
                                TT(dl["dS"].t[0:c, 0:c], dl["dec"].t[0:c, 0:c], strict.t[0:c, 0:c], ALU.mult, [dl["dec"], strict], [dl["dS"]])
                                STT(dl["Nm"].t[0:c, 0:c], pB.t[0:c, 0:c], tc(2), dl["dS"].t[0:c, 0:c], ALU.mult, ALU.mult, [pB, tcol, dl["dS"]], [dl["Nm"]])
                                pD = small(); pH = small(); pI = small()
                                TR(pD.t[0:c, 0:c], dl["Nm"].t[0:c, 0:c], id32.t[0:c, 0:c], [dl["Nm"], id32], [pD])
                                MM(pH.t[0:c, :], knb[:, cs], idb[:], [ctmp, idb], [pH])
                                MM(pI.t[0:c, :], vb[:, cs], idb[:], [pre[2], idb], [pI])
                                yield
                                CP("act", dl["NmT"].t[0:c, 0:c], pD.t[0:c, 0:c], [pD], [dl["NmT"]])
                                TT(dl["Y0"].t[0:c, 0:c], dl["Nm"].t[0:c, 0:c], id32.t[0:c, 0:c], ALU.add, [dl["Nm"], id32], [dl["Y0"]])
                                TSC(bv("kd")[0:c, :], pH.t[0:c, :], tc(4), None, ALU.mult, None, [pH, tcol], [dl["kd"]])
                                CP("act", dl["vtok"].t[0:c, :], pI.t[0:c, :], [pI], [dl["vtok"]])
                                yield

                                def series(Pm, PT, Y, nl):
                                    for lv in range(nl):
                                        P2 = dl["P0"] if lv % 2 == 0 else dl["P1"]
                                        T2 = dl["T0"] if lv % 2 == 0 else dl["T1"]
                                        Yn = dl["Y1"] if lv % 2 == 0 else dl["Y0"]
                                        pF = small()
                                        MM(pF.t[0:c, 0:c], Pm.t[0:c, 0:c], PT.t[0:c, 0:c], [PT, Pm], [pF])
                                        yield
                                        CP("dve", T2.t[0:c, 0:c], pF.t[0:c, 0:c], [pF], [T2])
                                        yield
                                        pG = small()
                                        MM(pG.t[0:c, 0:c], T2.t[0:c, 0:c], Y.t[0:c, 0:c], [T2, Y], [pG])
                                        if lv < nl - 1:
                                            pE = small()
                                            TR(pE.t[0:c, 0:c], T2.t[0:c, 0:c], id32.t[0:c, 0:c], [T2, id32], [pE])
                                        yield
                                        TT(Yn.t[0:c, 0:c], Y.t[0:c, 0:c], pG.t[0:c, 0:c], ALU.add, [Y, pG], [Yn])
                                        if lv < nl - 1:
                                            CP("act", P2.t[0:c, 0:c], pE.t[0:c, 0:c], [pE], [P2])
                                        Pm, PT, Y = P2, T2, Yn
                                    return Y

                                if c < 128:
                                    XTf = yield from series(dl["Nm"], dl["NmT"], dl["Y0"], nlev)
                                    yield
                                    CP("act", bv("P1")[0:c, 0:c], XTf.t[0:c, 0:c], [XTf], [dl["P1"]])
                                    XTv = bv("P1"); XTbuf = dl["P1"]
                                else:
                                    Nd, No, NdT = dl["dS"], dl["dt"], dl["NmT"]
                                    TT(Nd[:], dl["Nm"][:], bmask[:], ALU.mult, [dl["Nm"], bmask], [Nd])
                                    TT(No[:], dl["Nm"][:], Nd[:], ALU.subtract, [dl["Nm"], Nd], [No])
                                    TT(NdT[:], dl["NmT"][:], bmask[:], ALU.mult, [dl["NmT"], bmask], [NdT])
                                    TT(dl["Y0"][:], Nd[:], id32[:], ALU.add, [Nd, id32], [dl["Y0"]])
                                    yield
                                    Yd = yield from series(Nd, NdT, dl["Y0"], 4)
                                    YdT, Qm, QT, Z1, Q2T, XTb = dl["P0"], dl["T0"], dl["P1"], dl["Y1"], dl["T1"], dl["Nm"]
                                    yield
                                    pE = small()
                                    TR(pE[:], Yd[:], id32[:], [Yd, id32], [pE])
                                    yield
                                    CP("act", YdT[:], pE[:], [pE], [YdT])
                                    yield
                                    pF = small()
                                    MM(pF[:], No[:], YdT[:], [YdT, No], [pF])
                                    yield
                                    CP("dve", QT[:], pF[:], [pF], [QT])
                                    yield
                                    pG = small()
                                    MM(pG[:], QT[:], Yd[:], [QT, Yd], [pG])
                                    yield
                                    TT(Z1[:], Yd[:], pG[:], ALU.add, [Yd, pG], [Z1])
                                    yield
                                    pE = small()
                                    MM(pE[:], QT[:], Z1[:], [QT, Z1], [pE])
                                    yield
                                    CP("act", Qm[:], pE[:], [pE], [Qm])
                                    yield
                                    pG = small()
                                    MM(pG[:], QT[:], Qm[:], [QT, Qm], [pG])
                                    yield
                                    TT(bv("Nm"), Z1[:], pG[:], ALU.add, [Z1, pG], [XTb])
                                    XTv = bv("Nm"); XTbuf = XTb
                                yield
                                if nseq == 1:
                                    while sdone[0] != un:
                                        yield
                                pJ = small()
                                MM(pJ.t[0:c, :], knb[:, cs], Sb[:], [ctmp, Sb], [pJ])
                                yield
                                STT(bv("Rn")[0:c, :], pJ.t[0:c, :], tc(3), dl["vtok"].t[0:c, :], ALU.mult, ALU.subtract, [pJ, tcol, dl["vtok"]], [dl["Rn"]])
                                yield
                                pK = small()
                                MM(pK.t[0:c, :], XTv[0:c, 0:c], bv("Rn")[0:c, :], [XTbuf, dl["Rn"]], [pK])
                                yield
                                TSC(bv("vnew")[0:c, :], pK.t[0:c, :], tc(2), None, ALU.mult, None, [pK, tcol], [dl["vnew"]])
                                yield
                                pN = small()
                                if not so:
                                    pM = small(); pL = small()
                                    MM(pL.t[0:c, :], qnb[:, cs], Sb[:], [ctmp, Sb], [pL])
                                MM(pN[:], bv("kd")[0:c, :], bv("vnew")[0:c, :], [dl["kd"], dl["vnew"]], [pN])
                                if not so:
                                    MM(pM.t[0:c, :], bv("ATm")[0:c, 0:c], bv("vnew")[0:c, :], [dl["ATm"], dl["vnew"]], [pM])
                                yield
                                STT(S[:], S[:], egl.t[:, 0:1], pN[:], ALU.mult, ALU.add, [S, egl, pN], [S])
                                if nseq == 1:
                                    CP("act", Sb[:], S[:], [S], [Sb])
                                sdone[0] += 1
                                if so:
                                    return
                                CP("act", dl["P3"].t[0:c, :], pM.t[0:c, :], [pM], [dl["P3"]])
                                if nseq > 1:
                                    P.D("sp", deltas[un, hd], S[:], r=[S], is_output=True)
                                elif blk == nblk - 1 and un == nun - 1 and not so:
                                    P.D("sp", deltap[hd], S[:], r=[S], is_output=True)
                                yield
                                STT(dl["otok"].t[0:c, :], pL.t[0:c, :], tc(3), dl["P3"].t[0:c, :], ALU.mult, ALU.add, [pL, tcol, dl["P3"]], [dl["otok"]])
                                ACT(dl["sq"].t[0:c, :], dl["otok"].t[0:c, :], AF.Square, [dl["otok"]], [dl["sq"]])
                                yield
                                RSUM(dcols[0].t[0:c, :], dl["sq"].t[0:c, :], [dl["sq"]], [dcols[0]])
                                TSC(dcols[1].t[0:c, :], dcols[0].t[0:c, :], 1.0 / 128, EPS, ALU.mult, ALU.add, [dcols[0]], [dcols[1]])
                                ACT(dcols[2].t[0:c, :], dcols[1].t[0:c, :], AF.Sqrt, [dcols[1]], [dcols[2]])
                                yield
                                RECIP(dcols[3].t[0:c, :], dcols[2].t[0:c, :], [dcols[2]], [dcols[3]])
                                TSC(bv("on")[0:c, :], dl["otok"].t[0:c, :], dcols[3].t[0:c, 0:1], None, ALU.mult, None, [dl["otok"], dcols[3]], [dl["on"]])
                                pO = small()
                                MM(pO.t[:, 0:c], bv("on")[0:c, :], idb.t[0:c, 0:c], [dl["on"], idb], [pO])
                                yield
                                STT(mixin.t[:, hd, cs], pO.t[:, 0:c], gdc.t[:, 0:1], sz.t[:, cs], ALU.mult, ALU.mult, [pO, gdc, sz], [mixin])

                            pending = list(range(nun))
                            use_ext[0] = True
                            sdone = [0]
                            active = []
                            free_slots = list(range(NSLOT))
                            while pending or active:
                                while pending and free_slots:
                                    sl = free_slots.pop(0)
                                    active.append((sl, unit_gen(pending.pop(0), slots[sl][0], slots[sl][1], slots[sl][2])))
                                nxt = []
                                for sl, g in active:
                                    try:
                                        next(g)
                                        nxt.append((sl, g))
                                    except StopIteration:
                                        free_slots.append(sl)
                                active = nxt
                            use_ext[0] = False
                    P.fence()
                    ck(9)
                    with contextlib.ExitStack() as sc:
                        ubf = sb("ubf", [128, 8, 512], BF16, sc)
                        y5fm = sb("y5fm", [128, 8, 512], BF16, sc)
                        hb = [[sb(f"hb{q}{ri}", [128, 512], BF16, sc) for ri in range(2)] for q in range(4)]
                        t5 = [sb(f"t5_{i}", [128, 16], F32, sc) for i in range(2)]
                        gbig = [sb(f"gbig{i}", [128, 512], F32, sc) for i in range(2)]
                        bt0 = sb("bt0", [128, 512], F32, sc)
                        h32 = [sb(f"h32_{i}", [128, 144], F32, sc) for i in range(2)]
                        yv = sb("yv", [128, 512], F32, sc); y2 = sb("y2", [128, 512], F32, sc); y3 = sb("y3", [128, 512], F32, sc)
                        y5g = ubf
                        sst = sb("sst", [16, 128], F32, sc)
                        for fc in range(8):
                            p = proj(w_in, 4112 + fc * 128, 128, nb)
                            CP("act", ubf.t[:, fc, 0:nb], p.t[:, 0:nb], [p], [ubf])
                        if blk == 0 and nseq == 1 and first:
                            MEMSET(gin[0][:], 0.0, [gin[0]]); MEMSET(gin[1][:], 0.0, [gin[1]])
                        elif blk == 0 and nseq == 1:
                            for ri in range(2):
                                TSC(gin[ri][:], gin[ri][:], flg.t[:, 0:1], None, ALU.mult, None, [gin[ri], flg], [gin[ri]])
                        for fc in range(8):
                            for q in range(4):
                                ti = fc * 4 + q
                                pR = big(); pI_ = big()
                                MM(pR.t[:, 0:nb], Wb[0].t[:, ti, :], ubf.t[:, fc, 0:nb], [Wb[0], ubf], [pR])
                                MM(pI_.t[:, 0:nb], Wb[1].t[:, ti, :], ubf.t[:, fc, 0:nb], [Wb[1], ubf], [pI_])
                                if nseq == 1:
                                    C4 = tabc.t[:, ti, :].unsqueeze(1).broadcast_to([128, ntile, 128])
                                    S4 = tabs.t[:, ti, :].unsqueeze(1).broadcast_to([128, ntile, 128])
                                    v3 = lambda ap: ap.rearrange("p (c t) -> p c t", t=128)
                                    bR = v3(pR.t[:, 0:nb]); bI = v3(pI_.t[:, 0:nb])
                                    A0 = v3(yv.t[:, 0:nb]); A1 = v3(y2.t[:, 0:nb]); Gri = v3(y3.t[:, 0:nb]); Gii = v3(bt0.t[:, 0:nb])
                                    TT(A0, bR, C4, ALU.mult, [pR, tabc], [yv])
                                    TT(A1, bI, S4, ALU.mult, [pI_, tabs], [y2])
                                    TT(Gri, A0, A1, ALU.add, [yv, y2], [y3])
                                    TT(A0, bI, C4, ALU.mult, [pI_, tabc], [yv])
                                    TT(A1, bR, S4, ALU.mult, [pR, tabs], [y2])
                                    TT(Gii, A0, A1, ALU.subtract, [yv, y2], [bt0])
                                    magb = magc.t[:, ti:ti + 1].broadcast_to([128, 128])
                                    c128 = t128[0].t[:, ti:ti + 1]; s128 = t128[1].t[:, ti:ti + 1]
                                    for sc_i in range(ntile):
                                        cs = slice(sc_i * 128, sc_i * 128 + 128)
                                        SCAN(gbig[0].t[:, cs], magb, y3.t[:, cs], gin[0].t[:, ti:ti + 1], [magc, y3, gin[0]], [gbig[0]])
                                        SCAN(gbig[1].t[:, cs], magb, bt0.t[:, cs], gin[1].t[:, ti:ti + 1], [magc, bt0, gin[1]], [gbig[1]])
                                        gr = gbig[0].t[:, sc_i * 128 + 127:sc_i * 128 + 128]; gi = gbig[1].t[:, sc_i * 128 + 127:sc_i * 128 + 128]
                                        if blk == nblk - 1 and sc_i == ntile - 1 and not so:
                                            c1 = tabc.t[:, ti, 127:128]; s1 = tabs.t[:, ti, 127:128]
                                            TT(t5[0].t[:, 0:1], gi, s1, ALU.mult, [gbig[1], tabs], [t5[0]])
                                            STT(hfin[0].t[:, ti:ti + 1], gr, c1, t5[0].t[:, 0:1], ALU.mult, ALU.subtract, [gbig[0], tabc, t5[0]], [hfin[0]])
                                            TT(t5[0].t[:, 1:2], gi, c1, ALU.mult, [gbig[1], tabc], [t5[0]])
                                            STT(hfin[1].t[:, ti:ti + 1], gr, s1, t5[0].t[:, 1:2], ALU.mult, ALU.add, [gbig[0], tabs, t5[0]], [hfin[1]])
                                        else:
                                            TT(t5[0].t[:, 0:1], gi, s128, ALU.mult, [gbig[1], t128[1]], [t5[0]])
                                            STT(gin[0].t[:, ti:ti + 1], gr, c128, t5[0].t[:, 0:1], ALU.mult, ALU.subtract, [gbig[0], t128[0], t5[0]], [gin[0]])
                                            TT(t5[0].t[:, 1:2], gi, c128, ALU.mult, [gbig[1], t128[0]], [t5[0]])
                                            STT(gin[1].t[:, ti:ti + 1], gr, s128, t5[0].t[:, 1:2], ALU.mult, ALU.add, [gbig[0], t128[1], t5[0]], [gin[1]])
                                    if so:
                                        continue
                                    gR = v3(gbig[0].t[:, 0:nb]); gI = v3(gbig[1].t[:, 0:nb])
                                    TTP(A0, gR, C4, ALU.mult, [gbig[0], tabc], [yv])
                                    TTP(A1, gI, S4, ALU.mult, [gbig[1], tabs], [y2])
                                    TTP(Gri, gR, S4, ALU.mult, [gbig[0], tabs], [y3])
                                    TTP(Gii, gI, C4, ALU.mult, [gbig[1], tabc], [bt0])
                                    TT(v3(hb[q][0].t[:, 0:nb]), A0, A1, ALU.subtract, [yv, y2], [hb[q][0]])
                                    TT(v3(hb[q][1].t[:, 0:nb]), Gri, Gii, ALU.add, [y3, bt0], [hb[q][1]])
                                else:
                                    for ri, srcs in enumerate((sre, sim)):
                                        P.D("sp", sst[:], srcs[0:NSM, ti * 128:(ti + 1) * 128], w=[sst])
                                        pq = small()
                                        TR(pq.t[:, 0:16], sst[:], id32.t[0:16, 0:16], [sst, id32], [pq])
                                        CP("dve", h32[ri].t[:, 0:16], pq.t[:, 0:16], [pq], [h32[ri]])
                                    ar = abre.t[:, ti:ti + 1]; ai = abim.t[:, ti:ti + 1]; nai = nabim.t[:, ti:ti + 1]
                                    for t in range(TS_):
                                        pv = slice(16 * t, 16 * t + 16); cu = slice(16 * t + 16, 16 * t + 32)
                                        STT(t5[0].t[:, 0:16], h32[0].t[:, pv], ar, pR.t[:, pv], ALU.mult, ALU.add, [h32[0], abre, pR], [t5[0]])
                                        STT(t5[1].t[:, 0:16], h32[1].t[:, pv], ar, pI_.t[:, pv], ALU.mult, ALU.add, [h32[1], abre, pI_], [t5[1]])
                                        STT(h32[0].t[:, cu], h32[1].t[:, pv], nai, t5[0].t[:, 0:16], ALU.mult, ALU.add, [h32[1], nabim, t5[0]], [h32[0]])
                                        STT(h32[1].t[:, cu], h32[0].t[:, pv], ai, t5[1].t[:, 0:16], ALU.mult, ALU.add, [h32[0], abim, t5[1]], [h32[1]])
                                    for ri, dsts in enumerate((sres, sims)):
                                        CP("act", hb[q][ri].t[:, 0:128], h32[ri].t[:, 16:144], [h32[ri]], [hb[q][ri]])
                                        pq = small()
                                        TR(pq.t[0:16, :], h32[ri].t[:, 128:144], id32[:], [h32[ri], id32], [pq])
                                        CP("dve", sst[:], pq.t[0:16, :], [pq], [sst])
                                        P.D("sp", dsts[0:NSM, ti * 128:(ti + 1) * 128], sst[:], r=[sst], is_output=True)
                            if so:
                                continue
                            p = big()
                            for q in range(4):
                                ti = fc * 4 + q
                                MM(p.t[:, 0:nb], Wc[0].t[:, ti, :], hb[q][0].t[:, 0:nb], [Wc[0], hb[q][0]], [p], start=(q == 0), stop=False, inc=False)
                                MM(p.t[:, 0:nb], Wc[1].t[:, ti, :], hb[q][1].t[:, 0:nb], [Wc[1], hb[q][1]], [p], start=False, stop=(q == 3), inc=(q == 3))
                            STT(yv.t[:, 0:nb], ubf.t[:, fc, 0:nb], dsk.t[:, fc:fc + 1], p.t[:, 0:nb], ALU.mult, ALU.add, [ubf, dsk, p], [yv])
                            TT(y2.t[:, 0:nb], yv.t[:, 0:nb], yv.t[:, 0:nb], ALU.mult, [yv], [y2])
                            TSC(y2.t[:, 0:nb], y2.t[:, 0:nb], 0.044715, 1.0, ALU.mult, ALU.add, [y2], [y2])
                            TT(y2.t[:, 0:nb], y2.t[:, 0:nb], yv.t[:, 0:nb], ALU.mult, [y2, yv], [y2])
                            ACT(y3.t[:, 0:nb], y2.t[:, 0:nb], AF.Sigmoid, [y2], [y3], scale=2.0 * math.sqrt(2.0 / math.pi))
                            TT(y5fm.t[:, fc, 0:nb], yv.t[:, 0:nb], y3.t[:, 0:nb], ALU.mult, [yv, y3], [y5fm])
                        ck(10)
                        if so:
                            P.fence()
                            continue
                        pss_ = pbig[3]
                        for j in range(8):
                            pa = proj(w_glu, j * 128, 128, nb, nk=8, src=y5fm)
                            pg = proj(w_glu, 1024 + j * 128, 128, nb, nk=8, src=y5fm)
                            ACT(y3.t[:, 0:nb], pg.t[:, 0:nb], AF.Sigmoid, [pg], [y3])
                            TT(yv.t[:, 0:nb], pa.t[:, 0:nb], y3.t[:, 0:nb], ALU.mult, [pa, y3], [yv])
                            CP("act", y5g.t[:, j, 0:nb], yv.t[:, 0:nb], [yv], [y5g])
                            y2b = y2.t[:, :].bitcast(BF16)[:, 0:nb]
                            TT(y2b, yv.t[:, 0:nb], yv.t[:, 0:nb], ALU.mult, [yv], [y2])
                            MM(pss_.t[:, 0:nb], onesb[:], y2b, [onesb, y2], [pss_], start=(j == 0), stop=(j == 7), inc=True)
                        TSC(y2.t[:, 0:nb], pss_.t[:, 0:nb], 1.0 / 1024, EPS, ALU.mult, ALU.add, [pss_], [y2])
                        ACT(y3.t[:, 0:nb], y2.t[:, 0:nb], AF.Sqrt, [y2], [y3])
                        RECIP(yv.t[:, 0:nb], y3.t[:, 0:nb], [y3], [yv])
                        for j in range(8):
                            STT(mixin.t[:, 8 + j, 0:nb], y5g.t[:, j, 0:nb], g5.t[:, j:j + 1], yv.t[:, 0:nb], ALU.mult, ALU.mult, [y5g, g5, yv], [mixin])
                        ck(11)
                        pass
                    P.fence()
                    with contextlib.ExitStack() as sc:
                        mixst = sb("mixst", [128, 4, D], F32, sc)
                        build_gtrow(32, seg)
                        ofm = [sb(f"ofm{i}", [128, 512], F32, sc) for i in range(2)]
                        for cb in range(16):
                            wb = load_w(w_out, 0, 16, cb * 128, 128)
                            pf_ = big()
                            for kc in range(KC):
                                MM(pf_.t[:, 0:nb], wb.t[:, kc, :], mixin.t[:, kc, 0:nb], [wb, mixin], [pf_],
                                   start=(kc == 0), stop=(kc == KC - 1), inc=(kc == KC - 1))
                            of_ = ofm[cb % 2]
                            CP("act", of_.t[:, 0:nb], pf_.t[:, 0:nb], [pf_], [of_])
                            for tl in range(ntile):
                                pt_ = small()
                                TR(pt_[:], of_.t[:, tl * 128:(tl + 1) * 128], id32[:], [of_, id32], [pt_])
                                CP(cpeng(), mixst.t[:, tl, cb * 128:(cb + 1) * 128], pt_[:], [pt_], [mixst])
                        for tl in range(ntile):
                            finish_residual(seg, Buf(mixst.t[:, tl, :], "mixv", trk=mixst), xsrc[c0 + tl * 128:c0 + (tl + 1) * 128, :], [], gtrow[seg.name],
                                            x1d.t[ydst_off + c0 + tl * 128:ydst_off + c0 + (tl + 1) * 128, :], [x1d], False)
                    P.fence()

            fbuf = sqb

            def finish_residual(seg, fsrc, xsrc_ap, xsrcbufs, gt, dst_ap, dstbufs, is_out):
                P.D("sp", xt[:], xsrc_ap, r=xsrcbufs, w=[xt])
                ACT(resb[:], fsrc[:], AF.Square, [fsrc], [resb])
                RSUM(cols4[0][:], resb[:], [resb], [cols4[0]])
                TSC(cols4[1][:], cols4[0][:], 1.0 / D, EPS, ALU.mult, ALU.add, [cols4[0]], [cols4[1]])
                ACT(cols4[2][:], cols4[1][:], AF.Sqrt, [cols4[1]], [cols4[2]])
                RECIP(cols4[3][:], cols4[2][:], [cols4[2]], [cols4[3]])
                STT(resb[:], fsrc[:], cols4[3].t[:, 0:1], gt[:], ALU.mult, ALU.mult, [fsrc, cols4[3], gt], [resb])
                TT(resb[:], resb[:], xt[:], ALU.add, [resb, xt], [resb])
                P.D("sp", dst_ap, resb[:], r=[resb], w=dstbufs, is_output=is_out)

            flg = sb("flg", [128, 1], F32, msc)
            P.D("sp", flg[:], flag, w=[flg])
            mixer(PREs, xpre, 0, so=True, first=True)
            ck(4)
            mixer(PRs, xp, 0, so=False, first=False)
            for ri, dsts in enumerate((srep, simp)):
                p = small()
                TR(p.t[0:32, :], hfin[ri][:], id32[:], [hfin[ri], id32], [p])
                CP("dve", stiny.t[0:32, :], p.t[0:32, :], [p], [stiny])
                P.D("sp", dsts, stiny.t[0:32, :], r=[stiny], is_output=True)
            ck(13)
            mixer(SMs, xs, TP)
            P.fence()
            msc.close()

            ck(14)
            with contextlib.ExitStack() as sc:
                hfm2 = sb("hfm2", [128, 16, 640], BF16, sc)
                act = sb("act", [128, FCH, 640], BF16, sc)
                wd = [sb(f"wd{i}", [128, 512], BF16, sc) for i in range(4)]
                sg = sb("sg", [128, 640], F32, sc)
                gtr2 = sb("gtr2", [128, D], F32, sc)
                wbig = [sb(f"wbig{i}", [128, 16, 512], BF16, sc) for i in range(2)]
                fst = [sb(f"fst{i}", [128, 512], F32, sc) for i in range(2)]
                build_gtrow(80, PRs)
                gtrow["sm"] = gtr2
                build_gtrow(80, SMs)
                tiles = [(PRs, tl * 128, yp, tl * 128) for tl in range(TP // 128)] + [(SMs, TP, ys, 0)]
                nwb = [0]

                def load_wbig(W, c0):
                    nwb[0] += 1
                    b_ = wbig[nwb[0] % 2]
                    P.D("pool", b_.t[:, :, :], W[0:D, c0:c0 + 512].rearrange("(k p) n -> p k n", p=128), w=[b_])
                    return b_

                for blk_tiles in (tiles[0:5], tiles[5:9]):
                    nt = len(blk_tiles)
                    nb = nt * 128
                    for i, (seg, r0, yd, y0) in enumerate(blk_tiles):
                        make_h(seg, x1d.t[r0:r0 + 128, :], [x1d], 64, 48, i * 128, hdst=hfm2)
                    halves = [(0, nb // 2), (nb // 2, nb)] if nb > 512 else [(0, nb)]
                    for f4 in range(FCH // 4):
                        wg = load_wbig(w_gate, f4 * 512)
                        wu = load_wbig(w_up, f4 * 512)
                        for fi in range(4):
                            f = f4 * 4 + fi
                            for (a_, b_) in halves:
                                n_ = b_ - a_
                                pg = big()
                                for kc in range(KC):
                                    MM(pg.t[:, 0:n_], wg.t[:, kc, fi * 128:(fi + 1) * 128], hfm2.t[:, kc, a_:b_], [wg, hfm2], [pg],
                                       start=(kc == 0), stop=(kc == KC - 1), inc=(kc == KC - 1))
                                pu = big()
                                for kc in range(KC):
                                    MM(pu.t[:, 0:n_], wu.t[:, kc, fi * 128:(fi + 1) * 128], hfm2.t[:, kc, a_:b_], [wu, hfm2], [pu],
                                       start=(kc == 0), stop=(kc == KC - 1), inc=(kc == KC - 1))
                                ACT(sg.t[:, a_:b_], pg.t[:, 0:n_], AF.Silu, [pg], [sg])
                                TT(act.t[:, f, a_:b_], sg.t[:, a_:b_], pu.t[:, 0:n_], ALU.mult, [sg, pu], [act])
                    banks = [pbig[0], pbig[1], pbig[2], pbig[3], psm_t[0]]
                    for cb in range(4):
                        pbs = banks[:nt]
                        for f in range(FCH):
                            w_ = wd[f % 4]
                            P.D("pool", w_[:], w_down[f * 128:(f + 1) * 128, cb * 512:(cb + 1) * 512], w=[w_])
                            for tl in range(nt):
                                MM(pbs[tl][:], act.t[:, f, tl * 128:(tl + 1) * 128], w_[:], [act, w_], [pbs[tl]],
                                   start=(f == 0), stop=(f == FCH - 1), inc=True)
                        for tl, (seg, r0, yd, y0) in enumerate(blk_tiles):
                            fs_ = fst[tl % 2]
                            CP(cpeng(), fs_[:], pbs[tl][:], [pbs[tl]], [fs_])
                            P.D("sp", fd.t[r0:r0 + 128, cb * 512:(cb + 1) * 512], fs_[:], r=[fs_], w=[fd])
                    for i, (seg, r0, yd, y0) in enumerate(blk_tiles):
                        P.D("sp", sqb[:], fd.t[r0:r0 + 128, :], r=[fd], w=[sqb])
                        finish_residual(seg, sqb, x1d.t[r0:r0 + 128, :], [x1d], gtrow[seg.name],
                                        yd[y0:y0 + 128, :], [], True)

        except _Stop:
            pass
        DEAD[0] = False
        P.finish()
        with nc.Block() as block:
            P.emit(block)
    return nc, P


_CACHE = {}


def _consts():
    idn = np.eye(128, dtype=np.float32)
    j = np.arange(128)[:, None]; i = np.arange(128)[None, :]
    mneg = np.where(i >= j, 0.0, -30000.0).astype(np.float32)
    strict = (i > j).astype(np.float32)
    sel = np.zeros((8, 1024), np.float32)
    for h in range(8):
        sel[h, h * 128:(h + 1) * 128] = 1.0
    cm = np.zeros((128, 4, 128), np.float32)
    for q in range(4):
        for g2 in range(2):
            g8 = 2 * q + g2
            cm[g8 * 16:(g8 + 1) * 16, q, g2 * 64:(g2 + 1) * 64] = 1.0
    return idn, mneg, strict, sel, cm.reshape(128, 512)


def _bmask():
    a = np.arange(128) // 32
    return (a[:, None] == a[None, :]).astype(np.float32)


def kernel(x_prompt, x_sample, c_prompt, c_sample, state_conv, state_delta, state_ssm_re, state_ssm_im,
           w_ada, b_ada, g_pre_mix, g_post_mix, g_pre_ffn, g_post_ffn,
           w_in, w_conv, a_log, dt_bias, g_dn_out,
           lam_re, lam_im, log_step, b_re, b_im, c_re, c_im, d_skip,
           w_glu, g_s5_out, w_out, w_gate, w_up, w_down):
    f = lambda a: np.ascontiguousarray(np.asarray(a, dtype=np.float32))
    if "nc" not in _CACHE:
        _CACHE["nc"] = build_program()[0]
    nc = _CACHE["nc"]
    idn, mneg, strict, sel, cm = _consts()
    shared = {
        "w_ada": f(w_ada[0]), "b_ada": f(b_ada[0]).reshape(96, 128),
        "gains": f(np.concatenate([g_pre_mix[0], g_post_mix[0], g_pre_ffn[0], g_post_ffn[0]])).reshape(64, 128),
        "w_in": f(w_in[0]), "w_conv": f(w_conv[0]).reshape(96, 128),
        "hv": f(np.stack([a_log[0], dt_bias[0]], axis=1)), "gdn": f(g_dn_out[0]).reshape(1, 128),
        "lam_re": f(lam_re[0]).reshape(32, 128), "lam_im": f(lam_im[0]).reshape(32, 128),
        "lstep": f(log_step[0]).reshape(32, 2),
        "b_re": f(b_re[0]).reshape(4096, 16), "b_im": f(b_im[0]).reshape(4096, 16),
        "c_re": f(c_re[0]).reshape(1024, 64), "c_im": f(c_im[0]).reshape(1024, 64),
        "dskip": f(d_skip[0]).reshape(8, 128), "gs5": f(g_s5_out[0]).reshape(8, 128),
        "w_glu": f(w_glu[0]), "w_out": f(w_out[0]), "w_gate": f(w_gate[0]), "w_up": f(w_up[0]), "w_down": f(w_down[0]),
        "k_id": idn, "k_mneg": mneg, "k_strict": strict, "k_sel": sel, "k_cmask": cm, "k_bmask": _bmask(),
    }
    in_maps = []
    for c in range(8):
        sq = slice(16 * c, 16 * c + 16)
        m = dict(shared)
        sq_, hf_ = c // 2, c % 2
        m["xp"] = f(x_prompt[sq_, hf_ * TP:(hf_ + 1) * TP])
        m["xpre"] = f(x_prompt[sq_, 0:TP])
        m["flag"] = np.full((128, 1), float(hf_), np.float32)
        m["xs"] = f(np.transpose(x_sample[sq], (1, 0, 2)).reshape(128, D))
        m["call"] = f(np.concatenate([c_prompt[c // 2][None], c_sample[sq]], axis=0))
        m["sconv"] = f(np.transpose(state_conv[0][sq], (1, 0, 2)).reshape(48, QKV))
        m["sdelta"] = f(state_delta[0][sq])
        m["sre"] = f(state_ssm_re[0][sq]).reshape(16, 4096)
        m["sim"] = f(state_ssm_im[0][sq]).reshape(16, 4096)
        in_maps.append(m)
    res = run_bass_kernel_spmd(nc, in_maps, core_ids=list(range(8)))
    R = res.results
    yp = np.stack([np.concatenate([R[2 * b]["yp"], R[2 * b + 1]["yp"]], axis=0) for b in range(4)], axis=0)
    ys = np.concatenate([R[c]["ys"].reshape(8, 16, D).transpose(1, 0, 2) for c in range(8)], axis=0)
    convp = np.stack([R[2 * b + 1]["convp"] for b in range(4)], axis=0)[None]
    deltap = np.stack([R[2 * b + 1]["deltap"] for b in range(4)], axis=0)[None]
    srep = np.stack([R[2 * b + 1]["srep"].reshape(64, 64) for b in range(4)], axis=0)[None]
    simp = np.stack([R[2 * b + 1]["simp"].reshape(64, 64) for b in range(4)], axis=0)[None]
    convs = np.concatenate([R[c]["convs"].reshape(3, 16, QKV).transpose(1, 0, 2) for c in range(8)], axis=0)[None]
    deltas = np.concatenate([R[c]["deltas"] for c in range(8)], axis=0)[None]
    sres = np.concatenate([R[c]["sres"].reshape(16, 64, 64) for c in range(8)], axis=0)[None]
    sims = np.concatenate([R[c]["sims"].reshape(16, 64, 64) for c in range(8)], axis=0)[None]
    outs = (yp, ys, convp, deltap, srep, simp, convs, deltas, sres, sims)
    return tuple(np.ascontiguousarray(o.astype(np.float32)) for o in outs)
```

```python
import contextlib
import math
import numpy as np
import concourse.bass as bass
import concourse.mybir as mybir
from concourse.bass_utils import run_bass_kernel_spmd

F32 = mybir.dt.float32
BF16 = mybir.dt.bfloat16
AF = mybir.ActivationFunctionType
ALU = mybir.AluOpType
AX = mybir.AxisListType

D = 2048
KC = 16
PROJ = 5136
QKV = 3072
DFF = 5632
FCH = DFF // 128
EPS = 1e-6
TP = 1024
NSM = 16
TS_ = 8


class Buf:
    def __init__(self, t, name="", trk=None, excl=False):
        self.t = t
        self.name = name
        self.w = None
        self.r = []
        self.trk = trk or self
        self.excl = excl
        self.pend = False

    def __getitem__(self, k):
        return self.t[k]


class Prog:
    SEM_LIMIT = 30000

    def __init__(self, nc, sems):
        self.nc = nc
        self.free = list(sems)
        self.ops = {k: [] for k in ("pe", "act", "dve", "pool", "sp")}
        self.cnt = {}
        self.sem = {}
        for k in ("pe", "act", "dve", "pool"):
            self.sem[k] = self.free.pop()
            self.cnt[k] = 0
        self.waited = {k: {} for k in self.ops}
        self.dpool = {"sp": [[self.free.pop(), 0] for _ in range(16)],
                      "pool": [[self.free.pop(), 0] for _ in range(12)]}
        self.dnext = {"sp": 0, "pool": 0}
        self.out_tokens = []
        self.ninstr = 0

    def _wait(self, eng, tok):
        sem, val = tok
        if self.waited[eng].get(sem, 0) >= val:
            return
        self.waited[eng][sem] = val
        self.ops[eng].append(("w", sem, val))

    def _deps(self, eng, r, w):
        toks = []
        for b in r:
            if b.w is not None:
                toks.append(b.w)
        for b in w:
            if b.w is not None:
                toks.append(b.w)
            toks.extend(b.r)
        for tok in toks:
            if eng == "pe" and tok[0] is self.sem["pe"]:
                continue
            self._wait(eng, tok)

    def _roll(self, eng):
        if self.cnt[eng] >= self.SEM_LIMIT:
            self.sem[eng] = self.free.pop()
            self.cnt[eng] = 0

    @staticmethod
    def _norm(r, w):
        r2 = []
        w2 = [b.trk for b in w]
        for b in r:
            b = b.trk
            if b.excl:
                if b not in w2:
                    w2.append(b)
            else:
                r2.append(b)
        return r2, w2

    def E(self, eng, fn, r=(), w=(), inc=True):
        if DEAD[0]:
            return None
        for b in r:
            b.pend = False
        for b in w:
            b.pend = True
        r, w = self._norm(r, w)
        self._deps(eng, r, w)
        self.ninstr += 1
        self._roll(eng)
        if inc:
            self.cnt[eng] += 1
            tok = (self.sem[eng], self.cnt[eng])
            self.ops[eng].append(("i", fn, self.sem[eng]))
        else:
            tok = (self.sem[eng], self.cnt[eng] + 1)
            self.ops[eng].append(("i", fn, None))
        for b in r:
            b.r.append(tok)
            if len(b.r) > 64:
                b.r = b.r[-48:]
        for b in w:
            b.w = tok
            b.r = []
        return tok

    def D(self, q, out, in_, r=(), w=(), is_output=False, **kw):
        if DEAD[0]:
            return None
        r, w = self._norm(r, w)
        self._deps(q, r, w)
        self.ninstr += 1
        pool = self.dpool[q]
        i = self.dnext[q]
        self.dnext[q] = (i + 1) % len(pool)
        ent = pool[i]
        if ent[1] > 0:
            self._wait(q, (ent[0], ent[1]))
        ent[1] += 16
        tok = (ent[0], ent[1])
        self.ops[q].append(("d", out, in_, ent[0], kw))
        for b in r:
            b.r.append(tok)
        for b in w:
            b.w = tok
            b.r = []
        if is_output:
            self.out_tokens.append(tok)
        return tok

    def fence(self):
        if DEAD[0]:
            return
        toks = []
        for k in ("pe", "act", "dve", "pool"):
            if self.cnt[k] > 0:
                toks.append((self.sem[k], self.cnt[k]))
        for q in ("sp", "pool"):
            for ent in self.dpool[q]:
                if ent[1] > 0:
                    toks.append((ent[0], ent[1]))
        for eng in self.ops:
            for tok in toks:
                if eng in ("pe", "act", "dve", "pool") and tok[0] is self.sem[eng]:
                    continue
                self._wait(eng, tok)

    def finish(self):
        self.fence()

    def emit(self, block):
        ops = self.ops

        def run(e, lst):
            for op in lst:
                if op[0] == "w":
                    e.wait_ge(op[1], op[2])
                elif op[0] == "i":
                    ins = op[1](e)
                    if op[2] is not None:
                        ins.then_inc(op[2], 1)
                else:
                    e.dma_start(out=op[1], in_=op[2], **op[4]).then_inc(op[3], 16)

        @block.tensor
        def _(e):
            run(e, ops["pe"])

        @block.scalar
        def _(e):
            run(e, ops["act"])

        @block.vector
        def _(e):
            run(e, ops["dve"])

        @block.gpsimd
        def _(e):
            run(e, ops["pool"])

        @block.sync
        def _(e):
            run(e, ops["sp"])


STOP = [99]


class _Stop(Exception):
    pass


DEAD = [False]


def ck(k):
    if STOP[0] == k:
        DEAD[0] = True


class Seg:
    def __init__(self, name, T, nseq):
        self.name = name
        self.T = T
        self.nseq = nseq
        self.ncol = T * nseq


def build_program():
    nc = bass.Bass("TRN2", target_bir_lowering=False)
    di = {}

    def din(name, shape):
        di[name] = nc.dram_tensor(name, list(shape), F32, kind="ExternalInput").ap()
        return di[name]

    def dout(name, shape):
        di[name] = nc.dram_tensor(name, list(shape), F32, kind="ExternalOutput").ap()
        return di[name]

    xp = din("xp", [TP, D]); xs = din("xs", [128, D]); xpre = din("xpre", [TP, D]); flag = din("flag", [128, 1])
    call = din("call", [17, D])
    sconv = din("sconv", [48, QKV]); sdelta = din("sdelta", [NSM, 8, 128, 128])
    sre = din("sre", [NSM, 4096]); sim = din("sim", [NSM, 4096])
    w_ada = din("w_ada", [D, 6 * D]); b_ada = din("b_ada", [96, 128])
    gains = din("gains", [64, 128])
    w_in = din("w_in", [D, PROJ]); w_conv = din("w_conv", [96, 128])
    hv = din("hv", [8, 2])
    gdn = din("gdn", [1, 128])
    lam_re = din("lam_re", [32, 128]); lam_im = din("lam_im", [32, 128]); lstep = din("lstep", [32, 2])
    b_re = din("b_re", [4096, 16]); b_im = din("b_im", [4096, 16])
    c_re = din("c_re", [1024, 64]); c_im = din("c_im", [1024, 64])
    dskip = din("dskip", [8, 128]); gs5 = din("gs5", [8, 128])
    w_glu = din("w_glu", [1024, 2048]); w_out = din("w_out", [D, D])
    w_gate = din("w_gate", [D, DFF]); w_up = din("w_up", [D, DFF]); w_down = din("w_down", [DFF, D])
    k_id = din("k_id", [128, 128]); k_mneg = din("k_mneg", [128, 128]); k_strict = din("k_strict", [128, 128])
    k_sel = din("k_sel", [8, 1024]); k_cmask = din("k_cmask", [128, 512]); k_bmask = din("k_bmask", [128, 128])

    yp = dout("yp", [TP, D]); ys = dout("ys", [128, D])
    convp = dout("convp", [3, QKV]); deltap = dout("deltap", [8, 128, 128])
    srep = dout("srep", [32, 128]); simp = dout("simp", [32, 128])
    convs = dout("convs", [48, QKV]); deltas = dout("deltas", [NSM, 8, 128, 128])
    sres = dout("sres", [NSM, 4096]); sims = dout("sims", [NSM, 4096])
    x1d = Buf(nc.dram_tensor("x1d", [TP + 128, D], F32).ap(), "x1d")
    fd = Buf(nc.dram_tensor("fd", [TP + 128, D], F32).ap(), "fd")

    PRs = Seg("pr", TP, 1)
    PREs = Seg("pre", TP, 1)
    SMs = Seg("sm", TS_, NSM)

    with contextlib.ExitStack() as st:
        uid = [0]

        def sb(name, shape, dt=F32, stack=None):
            uid[0] += 1
            return Buf((stack or st).enter_context(nc.sbuf_tensor(f"{name}_{uid[0]}", list(shape), dt)), name)

        def ps(name, shape, dt=F32):
            return Buf(st.enter_context(nc.psum_tensor(name, list(shape), dt)), name, excl=True)

        sems = [st.enter_context(nc.semaphore(f"s{i}")) for i in range(96)]
        P = Prog(nc, sems)

        def MM(out, lhsT, rhs, r, w, start=True, stop=True, inc=True):
            P.E("pe", lambda e: e.matmul(out, lhsT, rhs, start=start, stop=stop), r, w, inc)

        def TR(out, in_, ident, r, w):
            P.E("pe", lambda e: e.transpose(out, in_, ident), r, w)

        def ACT(out, in_, func, r, w, scale=1.0):
            P.E("act", lambda e: e.activation(out=out, in_=in_, func=func, scale=scale), r, w)

        def TSC(out, in0, s1, s2, op0, op1, r, w):
            if s2 is None:
                P.E("dve", lambda e: e.tensor_scalar(out=out, in0=in0, scalar1=s1, scalar2=None, op0=op0), r, w)
            else:
                P.E("dve", lambda e: e.tensor_scalar(out=out, in0=in0, scalar1=s1, scalar2=s2, op0=op0, op1=op1), r, w)

        def TT(out, in0, in1, op, r, w):
            P.E("dve", lambda e: e.tensor_tensor(out=out, in0=in0, in1=in1, op=op), r, w)

        def TTP(out, in0, in1, op, r, w):
            P.E("pool", lambda e: e.tensor_tensor(out=out, in0=in0, in1=in1, op=op), r, w)

        def STT(out, in0, scalar, in1, op0, op1, r, w):
            P.E("dve", lambda e: e.scalar_tensor_tensor(out=out, in0=in0, scalar=scalar, in1=in1, op0=op0, op1=op1), r, w)

        def CP(eng, out, in_, r, w):
            if eng == "act":
                P.E("act", lambda e: e.activation(out=out, in_=in_, func=AF.Copy), r, w)
            else:
                P.E("dve", lambda e: e.tensor_copy(out=out, in_=in_), r, w)

        def MEMSET(ap, val, w):
            P.E("dve", lambda e: e.memset(ap, val), (), w)

        def RSUM(out, in_, r, w):
            P.E("dve", lambda e: e.reduce_sum(out=out, in_=in_, axis=AX.X), r, w)

        def RECIP(out, in_, r, w):
            P.E("dve", lambda e: e.reciprocal(out=out, in_=in_), r, w)

        def SCAN(out, d0, d1, init, r, w):
            P.E("dve", lambda e: e.tensor_tensor_scan(out=out, data0=d0, data1=d1, initial=init,
                                                      op0=ALU.mult, op1=ALU.add), r, w)

        pbig = [ps(f"pb{i}", [128, 512]) for i in range(4)]
        ptr_t = ps("ptr", [128, 1024], BF16)
        ptr = [Buf(ptr_t.t[:, i * 512:(i + 1) * 512], f"ptr{i}", trk=ptr_t) for i in range(2)]
        psm_t = [ps(f"psmt{i}", [128, 512]) for i in range(3)]
        psm = [Buf(psm_t[i % 3].t[:, (i // 3) * 128:(i // 3) * 128 + 128], f"psm{i}", trk=psm_t[i % 3]) for i in range(12)]
        ctr = {"big": 0, "tr": 0, "sm": 0, "w": 0, "cp": 0}

        def big():
            ctr["big"] += 1
            t_ = pbig[ctr["big"] % 3]
            assert not t_.pend, f"psum big tile {t_.name} reused while unread"
            return t_

        def ptrn():
            ctr["tr"] += 1
            return ptr[ctr["tr"] % 2]

        psm_ext = []
        for q_ in range(4):
            for b_ in range(7):
                if b_ < 3:
                    psm_ext.append(psm[q_ * 3 + b_])
                else:
                    psm_ext.append(Buf(pbig[b_ - 3].t[:, q_ * 128:(q_ + 1) * 128], f"psx{b_}{q_}", trk=pbig[b_ - 3]))
        use_ext = [False]

        def small():
            ctr["sm"] += 1
            if use_ext[0]:
                t_ = psm_ext[ctr["sm"] % 28]
                assert not t_.pend, f"psum ext tile {t_.name} reused while unread"
                return t_
            t_ = psm[ctr["sm"] % 12]
            assert not t_.pend, f"psum small tile {t_.name} reused while unread"
            return t_

        def cpeng():
            ctr["cp"] += 1
            return "act" if ctr["cp"] % 2 else "dve"

        id32 = sb("id32", [128, 128]); idb = sb("idb", [128, 128], BF16)
        mneg = sb("mneg", [128, 128]); strict = sb("strict", [128, 128])
        sel = sb("sel", [8, 1024]); bmask = sb("bmask", [128, 128])
        ones = sb("ones", [128, 128]); onesb = sb("onesb", [128, 128], BF16)
        P.D("sp", id32[:], k_id, w=[id32]); P.D("pool", idb[:], k_id, w=[idb])
        P.D("sp", mneg[:], k_mneg, w=[mneg]); P.D("sp", strict[:], k_strict, w=[strict])
        P.D("sp", sel[:], k_sel, w=[sel]); P.D("sp", bmask[:], k_bmask, w=[bmask])
        MEMSET(ones[:], 1.0, [ones]); MEMSET(onesb[:], 1.0, [onesb])

        wbufs = [sb(f"wch{i}", [128, 16, 128], BF16) for i in range(2)]

        def load_w(W, r0, nk, c0, ncols):
            ctr["w"] += 1
            b = wbufs[ctr["w"] % 2]
            src = W[r0:r0 + nk * 128, c0:c0 + ncols].rearrange("(k p) n -> p k n", p=128)
            P.D("pool", b.t[:, 0:nk, 0:ncols], src, w=[b])
            return b

        stg = sb("stg", [128, 128])

        def load_fm(dst, src_ap, n):
            P.D("sp", stg.t[0:n, :], src_ap, w=[stg])
            p = small()
            TR(p.t[:, 0:n], stg.t[0:n, :], id32.t[0:n, 0:n], [stg, id32], [p])
            CP("dve", dst, p.t[:, 0:n], [p], [])

        try:
            bada = sb("bada", [128, 96]); load_fm(bada[:], b_ada, 96); bada.w = None
            gn = sb("gn", [128, 64]); load_fm(gn[:], gains, 64)
            wcv = sb("wcv", [128, 96]); load_fm(wcv[:], w_conv, 96)
            dsk = sb("dsk", [128, 8]); load_fm(dsk[:], dskip, 8)
            g5 = sb("g5", [128, 8]); load_fm(g5[:], gs5, 8)
            gdc = sb("gdc", [128, 1]); load_fm(gdc[:], gdn, 1)
            hvb = sb("hvb", [8, 2]); P.D("sp", hvb[:], hv, w=[hvb])
            nA = sb("nA", [8, 1])
            ACT(nA[:], hvb.t[:, 0:1], AF.Exp, [hvb], [nA])
            TSC(nA[:], nA[:], -1.0, None, ALU.mult, None, [nA], [nA])
            P.fence()
            ck(1)

            mod = sb("mod", [128, 96, 17])

            ck(2)
            gtr1 = sb("gtrow", [128, D]); gtrow = {"pr": gtr1, "sm": gtr1}
            xreps = [sb("xrep", [128, 128]), sb("xrep2", [128, 128])]

            def build_gtrow(base, seg):
                for kc in range(KC):
                    xrep = xreps[kc % 2]
                    if seg.nseq == 1:
                        TSC(xrep[:], ones[:], mod.t[:, base + kc, 0:1], None, ALU.mult, None, [ones, mod], [xrep])
                        p = small()
                        TR(p[:], xrep[:], id32[:], [xrep, id32], [p])
                        CP("act", gtrow["pr"].t[:, kc * 128:(kc + 1) * 128], p[:], [p], [gtrow["pr"]])
                        continue
                    for t in range(TS_):
                        CP("dve", xrep.t[:, t * 16:(t + 1) * 16], mod.t[:, base + kc, 1:17], [mod], [xrep])
                    p = small()
                    TR(p[:], xrep[:], id32[:], [xrep, id32], [p])
                    CP("act", gtrow["sm"].t[:, kc * 128:(kc + 1) * 128], p[:], [p], [gtrow["sm"]])

            xt = sb("xt", [128, D]); sqb = sb("sqb", [128, D]); xn = sb("xn", [128, D], BF16)
            cols4 = [sb(f"col{i}", [128, 1]) for i in range(4)]
            hfm = sb("hfm", [128, 16, 512], BF16)
            hfm_g = hfm
            resb = sb("resb", [128, D])

            def rstd_of(src_ap, srcbufs, n, scale):
                ACT(sqb.t[:, 0:n], src_ap, AF.Square, srcbufs, [sqb])
                ck(41)
                RSUM(cols4[0][:], sqb.t[:, 0:n], [sqb], [cols4[0]])
                ck(42)
                TSC(cols4[1][:], cols4[0][:], scale, EPS, ALU.mult, ALU.add, [cols4[0]], [cols4[1]])
                ACT(cols4[2][:], cols4[1][:], AF.Sqrt, [cols4[1]], [cols4[2]])
                ck(43)
                RECIP(cols4[3][:], cols4[2][:], [cols4[2]], [cols4[3]])
                ck(44)
                return cols4[3]

            def make_h(seg, src_ap, srcbufs, scg, sh, col0, hdst=None):
                hfm = hdst or hfm_g
                P.D("sp", xt[:], src_ap, r=srcbufs, w=[xt])
                rs = rstd_of(xt[:], [xt], D, 1.0 / D)
                TSC(xn[:], xt[:], rs.t[:, 0:1], None, ALU.mult, None, [xt, rs], [xn])
                ck(45)
                for kc in range(KC):
                    p = ptrn()
                    TR(p.t[:, 0:128], xn.t[:, kc * 128:(kc + 1) * 128], idb[:], [xn, idb], [p])
                    if seg.nseq == 1:
                        TSC(hfm.t[:, kc, col0:col0 + 128], p.t[:, 0:128], mod.t[:, scg + kc, 0:1], mod.t[:, sh + kc, 0:1],
                            ALU.mult, ALU.add, [p, mod], [hfm])
                    else:
                        for s in range(NSM):
                            TSC(hfm.t[:, kc, col0 + s:col0 + 128:16], p.t[:, s:128:16], mod.t[:, scg + kc, 1 + s:2 + s],
                                mod.t[:, sh + kc, 1 + s:2 + s], ALU.mult, ALU.add, [p, mod], [hfm])

            def proj(W, col0, ncols, nb, nk=KC, src=None):
                src = src or hfm
                wb = load_w(W, 0, nk, col0, ncols)
                p = big()
                for kc in range(nk):
                    MM(p.t[0:ncols, 0:nb], wb.t[:, kc, 0:ncols], src.t[:, kc, 0:nb], [wb, src], [p],
                       start=(kc == 0), stop=(kc == nk - 1), inc=(kc == nk - 1))
                return p

            msc = contextlib.ExitStack()
            st.enter_context(msc)
            Wb = [sb("wb_re", [128, 32, 128], BF16, msc), sb("wb_im", [128, 32, 128], BF16, msc)]
            Wc = [sb("wc_re", [128, 32, 128], BF16, msc), sb("wc_im", [128, 32, 128], BF16, msc)]
            tabc = sb("tabc", [128, 32, 128], F32, msc); tabs = sb("tabs", [128, 32, 128], F32, msc)
            t128 = [sb("t128c", [128, 32], F32, msc), sb("t128s", [128, 32], F32, msc)]
            abre = sb("abre", [128, 32], F32, msc); abim = sb("abim", [128, 32], F32, msc); nabim = sb("nabim", [128, 32], F32, msc); magc = sb("magc", [128, 32], F32, msc)
            with contextlib.ExitStack() as sc:
                lre = sb("lre", [128, 32], F32, sc); lim = sb("lim", [128, 32], F32, sc); dts = sb("dts", [128, 32], F32, sc)
                ls2 = sb("ls2", [32, 2], F32, sc); lsx = sb("lsx", [32, 128], F32, sc)
                tA = [sb(f"s5t{i}", [128, 32], F32, sc) for i in range(8)]
                uc = sb("uc", [128, 32], F32, sc); us = sb("us", [128, 32], F32, sc)
                core = sb("core", [128, 32], F32, sc); coim = sb("coim", [128, 32], F32, sc)
                brt = sb("brt", [128, 32, 16], F32, sc); bit = sb("bit", [128, 32, 16], F32, sc)
                crt = sb("crt", [128, 8, 64], F32, sc); cit = sb("cit", [128, 8, 64], F32, sc)
                Z = sb("Z", [128, 128], F32, sc); t16 = sb("t16", [128, 16], F32, sc)
                hp = sb("hp", [128, 1], F32, sc)
                cmask = sb("cmask", [128, 512], F32, sc)
                P.D("sp", cmask[:], k_cmask, w=[cmask])
                load_fm(lre[:], lam_re, 32); lre.w = None
                load_fm(lim[:], lam_im, 32); lim.w = None
                P.D("sp", ls2[:], lstep, w=[ls2])
                TSC(lsx.t[:, 0:64], ones.t[0:32, 0:64], ls2.t[:, 0:1], None, ALU.mult, None, [ones, ls2], [lsx])
                TSC(lsx.t[:, 64:128], ones.t[0:32, 0:64], ls2.t[:, 1:2], None, ALU.mult, None, [ones, ls2], [lsx])
                p = small()
                TR(p.t[:, 0:32], lsx[:], id32.t[0:32, 0:32], [lsx, id32], [p])
                ACT(dts[:], p.t[:, 0:32], AF.Exp, [p], [dts])
                P.D("sp", brt[:], b_re.rearrange("(t p) h -> p t h", p=128), w=[brt])
                P.D("sp", bit[:], b_im.rearrange("(t p) h -> p t h", p=128), w=[bit])
                P.D("sp", crt[:], c_re.rearrange("(t p) h -> p t h", p=128), w=[crt])
                P.D("sp", cit[:], c_im.rearrange("(t p) h -> p t h", p=128), w=[cit])
                P.fence()
                TSC(lre[:], lre[:], -1e-4, None, ALU.min, None, [lre], [lre])
                TT(tA[0][:], lre[:], dts[:], ALU.mult, [lre, dts], [tA[0]])
                ACT(magc[:], tA[0][:], AF.Exp, [tA[0]], [magc])
                TT(tA[1][:], lim[:], dts[:], ALU.mult, [lim, dts], [tA[1]])
                MEMSET(hp[:], math.pi / 2, [hp])
                P.E("act", lambda e: e.activation(out=us[:], in_=tA[1][:], func=AF.Sin, scale=1.0 / 32), [tA[1]], [us])
                P.E("act", lambda e: e.activation(out=uc[:], in_=tA[1][:], func=AF.Sin, scale=1.0 / 32, bias=hp[:]), [tA[1], hp], [uc])

                def csquare(c, s):
                    TT(tA[2][:], c[:], c[:], ALU.mult, [c], [tA[2]])
                    TT(tA[3][:], s[:], s[:], ALU.mult, [s], [tA[3]])
                    TT(tA[4][:], c[:], s[:], ALU.mult, [c, s], [tA[4]])
                    TT(c[:], tA[2][:], tA[3][:], ALU.subtract, [tA[2], tA[3]], [c])
                    TSC(s[:], tA[4][:], 2.0, None, ALU.mult, None, [tA[4]], [s])

                for _ in range(5):
                    csquare(uc, us)
                TT(abre[:], magc[:], uc[:], ALU.mult, [magc, uc], [abre])
                TT(abim[:], magc[:], us[:], ALU.mult, [magc, us], [abim])
                TSC(nabim[:], abim[:], -1.0, None, ALU.mult, None, [abim], [nabim])
                TSC(tA[0][:], abre[:], -1.0, None, ALU.add, None, [abre], [tA[0]])
                TT(tA[1][:], lre[:], lre[:], ALU.mult, [lre], [tA[1]])
                TT(tA[2][:], lim[:], lim[:], ALU.mult, [lim], [tA[2]])
                TT(tA[1][:], tA[1][:], tA[2][:], ALU.add, [tA[1], tA[2]], [tA[1]])
                RECIP(tA[5][:], tA[1][:], [tA[1]], [tA[5]])
                TT(tA[2][:], tA[0][:], lre[:], ALU.mult, [tA[0], lre], [tA[2]])
                TT(tA[3][:], abim[:], lim[:], ALU.mult, [abim, lim], [tA[3]])
                TT(tA[2][:], tA[2][:], tA[3][:], ALU.add, [tA[2], tA[3]], [tA[2]])
                TT(core[:], tA[2][:], tA[5][:], ALU.mult, [tA[2], tA[5]], [core])
                TT(tA[2][:], abim[:], lre[:], ALU.mult, [abim, lre], [tA[2]])
                TT(tA[3][:], tA[0][:], lim[:], ALU.mult, [tA[0], lim], [tA[3]])
                TT(tA[2][:], tA[2][:], tA[3][:], ALU.subtract, [tA[2], tA[3]], [tA[2]])
                TT(coim[:], tA[2][:], tA[5][:], ALU.mult, [tA[2], tA[5]], [coim])
                pc = sb("pc", [128, 32], F32, sc); pss = sb("pss", [128, 32], F32, sc)
                CP("dve", pc[:], uc[:], [uc], [pc]); CP("dve", pss[:], us[:], [us], [pss])
                MEMSET(tabc.t[:, :, 0:1], 1.0, [tabc]); MEMSET(tabs.t[:, :, 0:1], 0.0, [tabs])
                tmpT = sb("tmpT", [128, 32, 64], F32, sc)
                n = 1
                while n < 128:
                    pcb = pc.t[:, :].unsqueeze(2).broadcast_to([128, 32, n])
                    psb = pss.t[:, :].unsqueeze(2).broadcast_to([128, 32, n])
                    tmp = tmpT.t[:, :, 0:n]
                    TT(tmp, tabs.t[:, :, 0:n], psb, ALU.mult, [tabs, pss], [tmpT])
                    TT(tabc.t[:, :, n:2 * n], tabc.t[:, :, 0:n], pcb, ALU.mult, [tabc, pc], [tabc])
                    TT(tabc.t[:, :, n:2 * n], tabc.t[:, :, n:2 * n], tmp, ALU.subtract, [tabc, tmpT], [tabc])
                    TT(tmp, tabs.t[:, :, 0:n], pcb, ALU.mult, [tabs, pc], [tmpT])
                    TT(tabs.t[:, :, n:2 * n], tabc.t[:, :, 0:n], psb, ALU.mult, [tabc, pss], [tabs])
                    TT(tabs.t[:, :, n:2 * n], tabs.t[:, :, n:2 * n], tmp, ALU.add, [tabs, tmpT], [tabs])
                    csquare(pc, pss)
                    n *= 2
                CP("dve", t128[0][:], pc[:], [pc], [t128[0]]); CP("dve", t128[1][:], pss[:], [pss], [t128[1]])
                def gen_w():
                    for ti in range(32):
                        q = ti % 4
                        for part, (ca, cb_, sgn) in enumerate(((core, coim, ALU.subtract), (core, coim, ALU.add))):
                            MEMSET(Z[:], 0.0, [Z])
                            for g2 in range(2):
                                rows = slice(g2 * 64, g2 * 64 + 64)
                                cs_ = slice((2 * q + g2) * 16, (2 * q + g2) * 16 + 16)
                                if part == 0:
                                    TSC(t16.t[rows, :], bit.t[rows, ti, :], coim.t[rows, ti:ti + 1], None, ALU.mult, None, [bit, coim], [t16])
                                    STT(Z.t[rows, cs_], brt.t[rows, ti, :], core.t[rows, ti:ti + 1], t16.t[rows, :], ALU.mult, ALU.subtract, [brt, core, t16], [Z])
                                else:
                                    TSC(t16.t[rows, :], brt.t[rows, ti, :], coim.t[rows, ti:ti + 1], None, ALU.mult, None, [brt, coim], [t16])
                                    STT(Z.t[rows, cs_], bit.t[rows, ti, :], core.t[rows, ti:ti + 1], t16.t[rows, :], ALU.mult, ALU.add, [bit, core, t16], [Z])
                            p = small()
                            TR(p[:], Z[:], id32[:], [Z, id32], [p])
                            CP("act", Wb[part].t[:, ti, :], p[:], [p], [Wb[part]])
                            yield
                    for ti in range(32):
                        q = ti % 4; fcx = ti // 4
                        for part, cc in enumerate((crt, cit)):
                            for g2 in range(2):
                                TT(Z.t[:, g2 * 64:(g2 + 1) * 64], cc.t[:, fcx, :], cmask.t[:, q * 128 + g2 * 64:q * 128 + (g2 + 1) * 64],
                                   ALU.mult, [cc, cmask], [Z])
                            p = small()
                            TR(p[:], Z[:], id32[:], [Z, id32], [p])
                            if part == 0:
                                CP("act", Wc[0].t[:, ti, :], p[:], [p], [Wc[0]])
                            else:
                                TSC(Wc[1].t[:, ti, :], p[:], -1.0, None, ALU.mult, None, [p], [Wc[1]])
                            yield

                with contextlib.ExitStack() as sc:
                    craw = sb("craw", [17, D], F32, sc); csl = sb("csl", [17, D], BF16, sc)
                    cT = sb("cT", [128, 16, 17], BF16, sc)
                    P.D("sp", craw[:], call, w=[craw])
                    ACT(csl[:], craw[:], AF.Silu, [craw], [csl])
                    for kc in range(KC):
                        p = ptrn()
                        TR(p.t[:, 0:17], csl.t[0:17, kc * 128:(kc + 1) * 128], idb.t[0:17, 0:17], [csl, idb], [p])
                        CP("dve", cT.t[:, kc, :], p.t[:, 0:17], [p], [cT])
                    gw = gen_w()
                    for f in range(96):
                        next(gw, None); next(gw, None)
                        wb = load_w(w_ada, 0, 16, f * 128, 128)
                        p = small()
                        for kc in range(KC):
                            MM(p.t[:, 0:17], wb.t[:, kc, :], cT.t[:, kc, :], [wb, cT], [p],
                               start=(kc == 0), stop=(kc == KC - 1), inc=(kc == KC - 1))
                        TSC(mod.t[:, f, :], p.t[:, 0:17], bada.t[:, f:f + 1], None, ALU.add, None, [p, bada], [mod])
                    for kc in range(KC):
                        TSC(mod.t[:, 16 + kc, :], mod.t[:, 16 + kc, :], 1.0, gn.t[:, kc:kc + 1], ALU.add, ALU.mult, [mod, gn], [mod])
                        TSC(mod.t[:, 32 + kc, :], mod.t[:, 32 + kc, :], gn.t[:, 16 + kc:17 + kc], None, ALU.mult, None, [mod, gn], [mod])
                        TSC(mod.t[:, 64 + kc, :], mod.t[:, 64 + kc, :], 1.0, gn.t[:, 32 + kc:33 + kc], ALU.add, ALU.mult, [mod, gn], [mod])
                        TSC(mod.t[:, 80 + kc, :], mod.t[:, 80 + kc, :], gn.t[:, 48 + kc:49 + kc], None, ALU.mult, None, [mod, gn], [mod])
                P.fence()
                for _ in gw:
                    pass
            P.fence()

            ck(3)
            mixin = sb("mixin", [128, 16, 512], BF16, msc)
            hist = sb("hist", [128, 24, 48], F32, msc)
            Sst = [sb(f"Sst{h}", [128, 128], F32, msc) for h in range(8)]
            gin = [sb("gin_re", [128, 32], F32, msc), sb("gin_im", [128, 32], F32, msc)]
            hfin = [sb("hfin_re", [128, 32], F32, msc), sb("hfin_im", [128, 32], F32, msc)]
            stiny = stg

            def mixer(seg, xsrc, ydst_off, so=False, first=True):
                nseq = seg.nseq
                hc = 3 * nseq
                nblk = (seg.ncol + 511) // 512
                c = 128 if nseq == 1 else TS_
                nlev = int(math.log2(c)) - 1
                build_state = (nseq > 1)
                for blk in range(nblk):
                    c0 = blk * 512
                    nb = min(512, seg.ncol - c0)
                    ntile = nb // 128
                    for tl in range(ntile):
                        make_h(seg, xsrc[c0 + tl * 128:c0 + (tl + 1) * 128, :], [], 16, 0, tl * 128)
                    ck(5)
                    with contextlib.ExitStack() as sc:
                        pre = [sb(f"pre{i}", [128, 48 + 512], F32, sc) for i in range(3)]
                        post = [sb(f"post{i}", [128, 512], F32, sc) for i in range(3)]
                        sz = sb("sz", [128, 512], F32, sc)
                        ctmp = sb("ctmp", [128, 512], F32, sc)
                        gf = sb("gf", [8, 512], F32, sc); gcum = sb("gcum", [8, 512], F32, sc)
                        beta = sb("beta", [8, 512], F32, sc); nbeta = sb("nbeta", [8, 512], F32, sc)
                        egc = sb("egc", [8, 512], F32, sc); gtmp = sb("gtmp", [8, 512], F32, sc); ekd = gtmp
                        nun = ntile if nseq == 1 else NSM
                        tcol = sb("tcol", [128, nun, 40], F32, sc)
                        dl = {k: sb("dl_" + k, [128, 128], F32, sc) for k in
                              ("dt", "dec", "dS", "Nm", "NmT", "Y0", "Y1", "P0", "P1", "T0", "T1", "ATm", "kd", "vtok",
                               "Rn", "vnew", "otok")}
                        dl["sq"] = dl["dt"]; dl["on"] = dl["dec"]; dl["P3"] = dl["dS"]
                        NSLOT = 5 if nseq > 1 else 4
                        sbf_t = sb("sbf", [128, 5, 128], BF16, sc)
                        sbf = [Buf(sbf_t.t[:, i, :], f"sbf{i}") for i in range(5)]
                        scols = sb("scols", [128, 25], F32, sc)
                        names17 = ("dt", "dec", "dS", "Nm", "NmT", "Y0", "Y1", "P0", "P1", "T0", "T1", "ATm", "kd", "vtok", "Rn", "vnew", "otok")
                        slots = []
                        for sl in range(NSLOT):
                            if sl == 0:
                                d_ = dl
                            elif sl == 4:
                                xnf = xn.t[:, :].bitcast(F32)
                                regs = [gtr1.t[:, i * 128:(i + 1) * 128] for i in range(16)] + [xnf[:, 7 * 128:8 * 128]]
                                d_ = {nm: Buf(regs[i], f"v{sl}{nm}") for i, nm in enumerate(names17)}
                            elif sl == 3:
                                xnf = xn.t[:, :].bitcast(F32)
                                regs = [resb.t[:, (6 + i) * 128:(7 + i) * 128] for i in range(10)] + [xnf[:, i * 128:(i + 1) * 128] for i in range(7)]
                                d_ = {nm: Buf(regs[i], f"v{sl}{nm}") for i, nm in enumerate(names17)}
                            else:
                                base_ = sqb if sl == 1 else xt
                                d_ = {nm: Buf(base_.t[:, i * 128:(i + 1) * 128], f"v{sl}{nm}") for i, nm in enumerate(names17[:16])}
                                d_["otok"] = Buf(resb.t[:, (sl - 1) * 128:sl * 128], f"v{sl}otok")
                            if sl > 0:
                                d_["sq"] = d_["dt"]; d_["on"] = d_["dec"]; d_["P3"] = d_["dS"]
                            eg_ = Buf(scols.t[:, 5 * sl:5 * sl + 1], f"egl{sl}")
                            dc_ = [Buf(scols.t[:, 5 * sl + 1 + i:5 * sl + 2 + i], f"dc{sl}{i}") for i in range(4)]
                            slots.append((d_, eg_, dc_))
                        Ssm = [Buf(resb.t[:, (2 + i) * 128:(3 + i) * 128], f"Ssm{i}") for i in range(4)] + [stg]
                        P.fence()
                        cst = sb("cst", [48, 128], F32, sc)
                        if blk == 0:
                            if nseq == 1 and first:
                                MEMSET(hist[:], 0.0, [hist])
                                for h in range(8):
                                    MEMSET(Sst[h][:], 0.0, [Sst[h]])
                            elif nseq == 1:
                                for ch in range(24):
                                    TSC(hist.t[:, ch, 0:3], hist.t[:, ch, 0:3], flg.t[:, 0:1], None, ALU.mult, None, [hist, flg], [hist])
                                for h in range(8):
                                    TSC(Sst[h][:], Sst[h][:], flg.t[:, 0:1], None, ALU.mult, None, [Sst[h], flg], [Sst[h]])
                            else:
                                for ch in range(24):
                                    P.D("sp", cst.t[0:48, :], sconv[0:48, ch * 128:(ch + 1) * 128], w=[cst])
                                    p = small()
                                    TR(p.t[:, 0:48], cst.t[0:48, :], id32.t[0:48, 0:48], [cst, id32], [p])
                                    CP("dve", hist.t[:, ch, 0:48], p.t[:, 0:48], [p], [hist])
                        ck(51)
                        p = proj(w_in, 4096, 8, nb)
                        TSC(gtmp.t[:, 0:nb], p.t[0:8, 0:nb], hvb.t[:, 1:2], None, ALU.add, None, [p, hvb], [gtmp])
                        ck(52)
                        ACT(gtmp.t[:, 0:nb], gtmp.t[:, 0:nb], AF.Exp, [gtmp], [gtmp])
                        TSC(gtmp.t[:, 0:nb], gtmp.t[:, 0:nb], 1.0, None, ALU.add, None, [gtmp], [gtmp])
                        ACT(gf.t[:, 0:nb], gtmp.t[:, 0:nb], AF.Ln, [gtmp], [gf])
                        TSC(gf.t[:, 0:nb], gf.t[:, 0:nb], nA.t[:, 0:1], None, ALU.mult, None, [gf, nA], [gf])
                        ck(53)
                        p = proj(w_in, 4104, 8, nb)
                        ACT(beta.t[:, 0:nb], p.t[0:8, 0:nb], AF.Sigmoid, [p], [beta])
                        TSC(nbeta.t[:, 0:nb], beta.t[:, 0:nb], -1.0, None, ALU.mult, None, [beta], [nbeta])
                        ck(54)
                        if nseq == 1:
                            for ci in range(ntile):
                                cs = slice(ci * 128, ci * 128 + 128)
                                SCAN(gcum.t[:, cs], ones.t[0:8, 0:128], gf.t[:, cs], 0.0, [ones, gf], [gcum])
                                TSC(gtmp.t[:, cs], gcum.t[:, cs], -1.0, gcum.t[:, ci * 128 + 127:ci * 128 + 128], ALU.mult, ALU.add, [gcum], [gtmp])
                        else:
                            CP("dve", gcum.t[:, 0:16], gf.t[:, 0:16], [gf], [gcum])
                            for t in range(1, TS_):
                                TT(gcum.t[:, 16 * t:16 * t + 16], gcum.t[:, 16 * t - 16:16 * t], gf.t[:, 16 * t:16 * t + 16], ALU.add, [gcum, gf], [gcum])
                            for t in range(TS_):
                                TT(gtmp.t[:, 16 * t:16 * t + 16], gcum.t[:, 112:128], gcum.t[:, 16 * t:16 * t + 16], ALU.subtract, [gcum], [gtmp])
                        ck(55)
                        ACT(gtmp.t[:, 0:nb], gtmp.t[:, 0:nb], AF.Exp, [gtmp], [gtmp])
                        ACT(egc.t[:, 0:nb], gcum.t[:, 0:nb], AF.Exp, [gcum], [egc])
                        ck(56)
                        for un in range(nun):
                            cs = slice(un * 128, un * 128 + 128) if nseq == 1 else slice(un, 128, 16)
                            p = small()
                            for k5, X in enumerate((gcum, beta, nbeta, egc, ekd)):
                                TR(p.t[0:c, 8 * k5:8 * k5 + 8], X.t[0:8, cs], id32.t[0:8, 0:8], [X, id32], [p])
                            CP("dve", tcol.t[0:c, un, :], p.t[0:c, 0:40], [p], [tcol])

                        ck(6)
                        for hd in range(8):
                            for wi in range(3):
                                ch = wi * 8 + hd
                                if so and wi == 0 and blk != nblk - 1:
                                    continue
                                p = proj(w_in, wi * 1024 + hd * 128, 128, nb)
                                CP("dve", pre[wi].t[:, 0:hc], hist.t[:, ch, 0:hc], [hist], [pre[wi]])
                                CP("act", pre[wi].t[:, hc:hc + nb], p.t[:, 0:nb], [p], [pre[wi]])
                                CP("dve", hist.t[:, ch, 0:hc], pre[wi].t[:, nb:nb + hc], [pre[wi]], [hist])
                                if so and wi == 0:
                                    continue
                                if blk == nblk - 1 and not so:
                                    pq = small()
                                    TR(pq.t[0:hc, :], pre[wi].t[:, nb:nb + hc], id32[:], [pre[wi], id32], [pq])
                                    CP("dve", cst.t[0:hc, :], pq.t[0:hc, :], [pq], [cst])
                                    dstc = (convp if nseq == 1 else convs)
                                    P.D("sp", dstc[0:hc, ch * 128:(ch + 1) * 128], cst.t[0:hc, :], r=[cst], is_output=True)
                                TSC(ctmp.t[:, 0:nb], pre[wi].t[:, 0:nb], wcv.t[:, ch:ch + 1], None, ALU.mult, None, [pre[wi], wcv], [ctmp])
                                for j in range(1, 4):
                                    STT(ctmp.t[:, 0:nb], pre[wi].t[:, j * nseq:j * nseq + nb], wcv.t[:, j * 24 + ch:j * 24 + ch + 1],
                                        ctmp.t[:, 0:nb], ALU.mult, ALU.add, [pre[wi], wcv, ctmp], [ctmp])
                                ACT(post[wi].t[:, 0:nb], ctmp.t[:, 0:nb], AF.Silu, [ctmp], [post[wi]])
                                if wi < 2:
                                    sqv = ctmp.t[:, :].bitcast(BF16)[:, 0:nb]
                                    TT(sqv, post[wi].t[:, 0:nb], post[wi].t[:, 0:nb], ALU.mult, [post[wi]], [ctmp])
                                    p = big()
                                    MM(p.t[:, 0:nb], onesb[:], sqv, [onesb, ctmp], [p])
                                    TSC(ctmp.t[:, 0:nb], p.t[:, 0:nb], 1e-6, None, ALU.add, None, [p], [ctmp])
                                    ACT(ctmp.t[:, 0:nb], ctmp.t[:, 0:nb], AF.Sqrt, [ctmp], [ctmp])
                                    RECIP(sz.t[:, 0:nb], ctmp.t[:, 0:nb], [ctmp], [sz])
                                    STT(post[wi].t[:, 0:nb], post[wi].t[:, 0:nb], (128.0 ** -0.5) if wi == 0 else 1.0, sz.t[:, 0:nb],
                                        ALU.mult, ALU.mult, [post[wi], sz], [post[wi]])
                            if not so:
                                p = proj(w_in, 3072 + hd * 128, 128, nb)
                                ACT(sz.t[:, 0:nb], p.t[:, 0:nb], AF.Silu, [p], [sz])
                            qn, kn, vv = post[0], post[1], post[2]
                            cbf = ctmp.t[:, :].bitcast(BF16)
                            knb = cbf[:, 0:nb]; qnb = cbf[:, 512:512 + nb]
                            CP("act", knb, kn.t[:, 0:nb], [kn], [ctmp])
                            if not so:
                                CP("act", qnb, qn.t[:, 0:nb], [qn], [ctmp])
                            ck(7)
                            def unit_gen(un, dl, egl, dcols):
                                cs = slice(un * 128, un * 128 + 128) if nseq == 1 else slice(un, 128, 16)
                                tc = lambda k: tcol.t[0:c, un, k * 8 + hd:k * 8 + hd + 1]
                                if nseq == 1:
                                    S = Sst[hd]
                                    Sb = sbf[0]
                                else:
                                    S = Ssm[un % 5]
                                    Sb = sbf[un % 5]
                                    P.D("sp", S[:], sdelta[un, hd], w=[S])
                                if nseq > 1 or un == 0:
                                    CP("act", Sb[:], S[:], [S], [Sb])
                                bv = lambda nm: dl[nm].t[:, :].bitcast(BF16)[:, 0:128]
                                pA = small(); pB = small(); pC = small()
                                MM(pA.t[:, 0:c], sel.t[0:8, hd * 128:(hd + 1) * 128], gcum.t[0:8, cs], [sel, gcum], [pA])
                                MM(pB.t[0:c, 0:c], knb[:, cs], knb[:, cs], [ctmp], [pB])
                                if not so:
                                    MM(pC.t[0:c, 0:c], knb[:, cs], qnb[:, cs], [ctmp], [pC])
                                else:
                                    pC.pend = False
                                yield
                                STT(dl["dt"].t[0:c, 0:c], pA.t[0:c, 0:c], tc(0), mneg.t[0:c, 0:c], ALU.subtract, ALU.add, [pA, tcol, mneg], [dl["dt"]])
                                ACT(egl[:], pA.t[:, c - 1:c], AF.Exp, [pA], [egl])
                                ACT(dl["dec"].t[0:c, 0:c], dl["dt"].t[0:c, 0:c], AF.Exp, [dl["dt"]], [dl["dec"]])
                                yield
                                if not so:
                                    TT(bv("ATm")[0:c, 0:c], pC.t[0:c, 0:c], dl["dec"].t[0:c, 0:c], ALU.mult, [pC, dl["dec"]], [dl["ATm"]])
                                TT(dl["dS"].t[0:c, 0:c], dl["dec"].t[0:c, 0:c], strict.t[0:c, 0:c], ALU.mult, [dl["dec"], strict], [dl["dS"]])
                                STT(dl["Nm"].t[0:c, 0:c], pB.t[0:c, 0:c], tc(2), dl["dS"].t[0:c, 0:c], ALU.mult, ALU.mult, [pB, tcol, dl["dS"]], [dl["Nm"]])
                                pD = small(); pH = small(); pI = small()
                                TR(pD.t[0:c, 0:c], dl["Nm"].t[0:c, 0:c], id32.t[0:c, 0:c], [dl["Nm"], id32], [pD])
                                TR(pH.t[0:c, :], kn.t[:, cs], id32[:], [kn, id32], [pH])
                                TR(pI.t[0:c, :], vv.t[:, cs], id32[:], [vv, id32], [pI])
                                yield
                                CP("act", dl["NmT"].t[0:c, 0:c], pD.t[0:c, 0:c], [pD], [dl["NmT"]])
                                TT(dl["Y0"].t[0:c, 0:c], dl["Nm"].t[0:c, 0:c], id32.t[0:c, 0:c], ALU.add, [dl["Nm"], id32], [dl["Y0"]])
                                TSC(bv("kd")[0:c, :], pH.t[0:c, :], tc(4), None, ALU.mult, None, [pH, tcol], [dl["kd"]])
                                CP("act", dl["vtok"].t[0:c, :], pI.t[0:c, :], [pI], [dl["vtok"]])
                                yield

                                def series(Pm, PT, Y, nl):
                                    for lv in range(nl):
                                        P2 = dl["P0"] if lv % 2 == 0 else dl["P1"]
                                        T2 = dl["T0"] if lv % 2 == 0 else dl["T1"]
                                        Yn = dl["Y1"] if lv % 2 == 0 else dl["Y0"]
                                        pF = small()
                                        MM(pF.t[0:c, 0:c], Pm.t[0:c, 0:c], PT.t[0:c, 0:c], [PT, Pm], [pF])
                                        yield
                                        CP("dve", T2.t[0:c, 0:c], pF.t[0:c, 0:c], [pF], [T2])
                                        yield
                                        pG = small()
                                        MM(pG.t[0:c, 0:c], T2.t[0:c, 0:c], Y.t[0:c, 0:c], [T2, Y], [pG])
                                        if lv < nl - 1:
                                            pE = small()
                                            TR(pE.t[0:c, 0:c], T2.t[0:c, 0:c], id32.t[0:c, 0:c], [T2, id32], [pE])
                                        yield
                                        TT(Yn.t[0:c, 0:c], Y.t[0:c, 0:c], pG.t[0:c, 0:c], ALU.add, [Y, pG], [Yn])
                                        if lv < nl - 1:
                                            CP("act", P2.t[0:c, 0:c], pE.t[0:c, 0:c], [pE], [P2])
                                        Pm, PT, Y = P2, T2, Yn
                                    return Y

                                if c < 128:
                                    XTf = yield from series(dl["Nm"], dl["NmT"], dl["Y0"], nlev)
                                    yield
                                    CP("act", bv("P1")[0:c, 0:c], XTf.t[0:c, 0:c], [XTf], [dl["P1"]])
                                    XTv = bv("P1"); XTbuf = dl["P1"]
                                else:
                                    Nd, No, NdT = dl["dS"], dl["dt"], dl["NmT"]
                                    TT(Nd[:], dl["Nm"][:], bmask[:], ALU.mult, [dl["Nm"], bmask], [Nd])
                                    TT(No[:], dl["Nm"][:], Nd[:], ALU.subtract, [dl["Nm"], Nd], [No])
                                    TT(NdT[:], dl["NmT"][:], bmask[:], ALU.mult, [dl["NmT"], bmask], [NdT])
                                    TT(dl["Y0"][:], Nd[:], id32[:], ALU.add, [Nd, id32], [dl["Y0"]])
                                    yield
                                    Yd = yield from series(Nd, NdT, dl["Y0"], 4)
                                    YdT, Qm, QT, Z1, Q2T, XTb = dl["P0"], dl["T0"], dl["P1"], dl["Y1"], dl["T1"], dl["Nm"]
                                    yield
                                    pE = small()
                                    TR(pE[:], Yd[:], id32[:], [Yd, id32], [pE])
                                    yield
                                    CP("act", YdT[:], pE[:], [pE], [YdT])
                                    yield
                                    pF = small()
                                    MM(pF[:], No[:], YdT[:], [YdT, No], [pF])
                                    yield
                                    CP("dve", QT[:], pF[:], [pF], [QT])
                                    yield
                                    pG = small()
                                    MM(pG[:], QT[:], Yd[:], [QT, Yd], [pG])
                                    yield
                                    TT(Z1[:], Yd[:], pG[:], ALU.add, [Yd, pG], [Z1])
                                    yield
                                    pE = small()
                                    MM(pE[:], QT[:], Z1[:], [QT, Z1], [pE])
                                    yield
                                    CP("act", Qm[:], pE[:], [pE], [Qm])
                                    yield
                                    pG = small()
                                    MM(pG[:], QT[:], Qm[:], [QT, Qm], [pG])
                                    yield
                                    TT(bv("Nm"), Z1[:], pG[:], ALU.add, [Z1, pG], [XTb])
                                    XTv = bv("Nm"); XTbuf = XTb
                                yield
                                if nseq == 1:
                                    while sdone[0] != un:
                                        yield
                                pJ = small()
                                MM(pJ.t[0:c, :], knb[:, cs], Sb[:], [ctmp, Sb], [pJ])
                                yield
                                STT(bv("Rn")[0:c, :], pJ.t[0:c, :], tc(3), dl["vtok"].t[0:c, :], ALU.mult, ALU.subtract, [pJ, tcol, dl["vtok"]], [dl["Rn"]])
                                yield
                                pK = small()
                                MM(pK.t[0:c, :], XTv[0:c, 0:c], bv("Rn")[0:c, :], [XTbuf, dl["Rn"]], [pK])
                                yield
                                TSC(bv("vnew")[0:c, :], pK.t[0:c, :], tc(2), None, ALU.mult, None, [pK, tcol], [dl["vnew"]])
                                yield
                                pN = small()
                                if not so:
                                    pM = small(); pL = small()
                                    MM(pL.t[0:c, :], qnb[:, cs], Sb[:], [ctmp, Sb], [pL])
                                MM(pN[:], bv("kd")[0:c, :], bv("vnew")[0:c, :], [dl["kd"], dl["vnew"]], [pN])
                                if not so:
                                    MM(pM.t[0:c, :], bv("ATm")[0:c, 0:c], bv("vnew")[0:c, :], [dl["ATm"], dl["vnew"]], [pM])
                                yield
                                STT(S[:], S[:], egl.t[:, 0:1], pN[:], ALU.mult, ALU.add, [S, egl, pN], [S])
                                if nseq == 1:
                                    CP("act", Sb[:], S[:], [S], [Sb])
                                sdone[0] += 1
                                if so:
                                    return
                                CP("act", dl["P3"].t[0:c, :], pM.t[0:c, :], [pM], [dl["P3"]])
                                if nseq > 1:
                                    P.D("sp", deltas[un, hd], S[:], r=[S], is_output=True)
                                elif blk == nblk - 1 and un == nun - 1 and not so:
                                    P.D("sp", deltap[hd], S[:], r=[S], is_output=True)
                                yield
                                STT(dl["otok"].t[0:c, :], pL.t[0:c, :], tc(3), dl["P3"].t[0:c, :], ALU.mult, ALU.add, [pL, tcol, dl["P3"]], [dl["otok"]])
                                ACT(dl["sq"].t[0:c, :], dl["otok"].t[0:c, :], AF.Square, [dl["otok"]], [dl["sq"]])
                                yield
                                RSUM(dcols[0].t[0:c, :], dl["sq"].t[0:c, :], [dl["sq"]], [dcols[0]])
                                TSC(dcols[1].t[0:c, :], dcols[0].t[0:c, :], 1.0 / 128, EPS, ALU.mult, ALU.add, [dcols[0]], [dcols[1]])
                                ACT(dcols[2].t[0:c, :], dcols[1].t[0:c, :], AF.Sqrt, [dcols[1]], [dcols[2]])
                                yield
                                RECIP(dcols[3].t[0:c, :], dcols[2].t[0:c, :], [dcols[2]], [dcols[3]])
                                TSC(dl["on"].t[0:c, :], dl["otok"].t[0:c, :], dcols[3].t[0:c, 0:1], None, ALU.mult, None, [dl["otok"], dcols[3]], [dl["on"]])
                                pO = small()
                                TR(pO.t[:, 0:c], dl["on"].t[0:c, :], id32.t[0:c, 0:c], [dl["on"], id32], [pO])
                                yield
                                STT(mixin.t[:, hd, cs], pO.t[:, 0:c], gdc.t[:, 0:1], sz.t[:, cs], ALU.mult, ALU.mult, [pO, gdc, sz], [mixin])

                            pending = list(range(nun))
                            use_ext[0] = True
                            sdone = [0]
                            active = []
                            free_slots = list(range(NSLOT))
                            while pending or active:
                                while pending and free_slots:
                                    sl = free_slots.pop(0)
                                    active.append((sl, unit_gen(pending.pop(0), slots[sl][0], slots[sl][1], slots[sl][2])))
                                nxt = []
                                for sl, g in active:
                                    try:
                                        next(g)
                                        nxt.append((sl, g))
                                    except StopIteration:
                                        free_slots.append(sl)
                                active = nxt
                            use_ext[0] = False
                    P.fence()
                    ck(9)
                    with contextlib.ExitStack() as sc:
                        ubf = sb("ubf", [128, 8, 512], BF16, sc)
                        y5fm = sb("y5fm", [128, 8, 512], BF16, sc)
                        hb = [[sb(f"hb{q}{ri}", [128, 512], BF16, sc) for ri in range(2)] for q in range(4)]
                        t5 = [sb(f"t5_{i}", [128, 16], F32, sc) for i in range(2)]
                        gbig = [sb(f"gbig{i}", [128, 512], F32, sc) for i in range(2)]
                        bt0 = sb("bt0", [128, 512], F32, sc)
                        h32 = [sb(f"h32_{i}", [128, 144], F32, sc) for i in range(2)]
                        yv = sb("yv", [128, 512], F32, sc); y2 = sb("y2", [128, 512], F32, sc); y3 = sb("y3", [128, 512], F32, sc)
                        y5g = ubf
                        sst = sb("sst", [16, 128], F32, sc)
                        for fc in range(8):
                            p = proj(w_in, 4112 + fc * 128, 128, nb)
                            CP("act", ubf.t[:, fc, 0:nb], p.t[:, 0:nb], [p], [ubf])
                        if blk == 0 and nseq == 1 and first:
                            MEMSET(gin[0][:], 0.0, [gin[0]]); MEMSET(gin[1][:], 0.0, [gin[1]])
                        elif blk == 0 and nseq == 1:
                            for ri in range(2):
                                TSC(gin[ri][:], gin[ri][:], flg.t[:, 0:1], None, ALU.mult, None, [gin[ri], flg], [gin[ri]])
                        for fc in range(8):
                            for q in range(4):
                                ti = fc * 4 + q
                                pR = big(); pI_ = big()
                                MM(pR.t[:, 0:nb], Wb[0].t[:, ti, :], ubf.t[:, fc, 0:nb], [Wb[0], ubf], [pR])
                                MM(pI_.t[:, 0:nb], Wb[1].t[:, ti, :], ubf.t[:, fc, 0:nb], [Wb[1], ubf], [pI_])
                                if nseq == 1:
                                    C4 = tabc.t[:, ti, :].unsqueeze(1).broadcast_to([128, ntile, 128])
                                    S4 = tabs.t[:, ti, :].unsqueeze(1).broadcast_to([128, ntile, 128])
                                    v3 = lambda ap: ap.rearrange("p (c t) -> p c t", t=128)
                                    bR = v3(pR.t[:, 0:nb]); bI = v3(pI_.t[:, 0:nb])
                                    A0 = v3(yv.t[:, 0:nb]); A1 = v3(y2.t[:, 0:nb]); Gri = v3(y3.t[:, 0:nb]); Gii = v3(bt0.t[:, 0:nb])
                                    TT(A0, bR, C4, ALU.mult, [pR, tabc], [yv])
                                    TT(A1, bI, S4, ALU.mult, [pI_, tabs], [y2])
                                    TT(Gri, A0, A1, ALU.add, [yv, y2], [y3])
                                    TT(A0, bI, C4, ALU.mult, [pI_, tabc], [yv])
                                    TT(A1, bR, S4, ALU.mult, [pR, tabs], [y2])
                                    TT(Gii, A0, A1, ALU.subtract, [yv, y2], [bt0])
                                    magb = magc.t[:, ti:ti + 1].broadcast_to([128, 128])
                                    c128 = t128[0].t[:, ti:ti + 1]; s128 = t128[1].t[:, ti:ti + 1]
                                    for sc_i in range(ntile):
                                        cs = slice(sc_i * 128, sc_i * 128 + 128)
                                        SCAN(gbig[0].t[:, cs], magb, y3.t[:, cs], gin[0].t[:, ti:ti + 1], [magc, y3, gin[0]], [gbig[0]])
                                        SCAN(gbig[1].t[:, cs], magb, bt0.t[:, cs], gin[1].t[:, ti:ti + 1], [magc, bt0, gin[1]], [gbig[1]])
                                        gr = gbig[0].t[:, sc_i * 128 + 127:sc_i * 128 + 128]; gi = gbig[1].t[:, sc_i * 128 + 127:sc_i * 128 + 128]
                                        if blk == nblk - 1 and sc_i == ntile - 1 and not so:
                                            c1 = tabc.t[:, ti, 127:128]; s1 = tabs.t[:, ti, 127:128]
                                            TT(t5[0].t[:, 0:1], gi, s1, ALU.mult, [gbig[1], tabs], [t5[0]])
                                            STT(hfin[0].t[:, ti:ti + 1], gr, c1, t5[0].t[:, 0:1], ALU.mult, ALU.subtract, [gbig[0], tabc, t5[0]], [hfin[0]])
                                            TT(t5[0].t[:, 1:2], gi, c1, ALU.mult, [gbig[1], tabc], [t5[0]])
                                            STT(hfin[1].t[:, ti:ti + 1], gr, s1, t5[0].t[:, 1:2], ALU.mult, ALU.add, [gbig[0], tabs, t5[0]], [hfin[1]])
                                        else:
                                            TT(t5[0].t[:, 0:1], gi, s128, ALU.mult, [gbig[1], t128[1]], [t5[0]])
                                            STT(gin[0].t[:, ti:ti + 1], gr, c128, t5[0].t[:, 0:1], ALU.mult, ALU.subtract, [gbig[0], t128[0], t5[0]], [gin[0]])
                                            TT(t5[0].t[:, 1:2], gi, c128, ALU.mult, [gbig[1], t128[0]], [t5[0]])
                                            STT(gin[1].t[:, ti:ti + 1], gr, s128, t5[0].t[:, 1:2], ALU.mult, ALU.add, [gbig[0], t128[1], t5[0]], [gin[1]])
                                    if so:
                                        continue
                                    gR = v3(gbig[0].t[:, 0:nb]); gI = v3(gbig[1].t[:, 0:nb])
                                    TTP(A0, gR, C4, ALU.mult, [gbig[0], tabc], [yv])
                                    TTP(A1, gI, S4, ALU.mult, [gbig[1], tabs], [y2])
                                    TTP(Gri, gR, S4, ALU.mult, [gbig[0], tabs], [y3])
                                    TTP(Gii, gI, C4, ALU.mult, [gbig[1], tabc], [bt0])
                                    TT(v3(hb[q][0].t[:, 0:nb]), A0, A1, ALU.subtract, [yv, y2], [hb[q][0]])
                                    TT(v3(hb[q][1].t[:, 0:nb]), Gri, Gii, ALU.add, [y3, bt0], [hb[q][1]])
                                else:
                                    for ri, srcs in enumerate((sre, sim)):
                                        P.D("sp", sst[:], srcs[0:NSM, ti * 128:(ti + 1) * 128], w=[sst])
                                        pq = small()
                                        TR(pq.t[:, 0:16], sst[:], id32.t[0:16, 0:16], [sst, id32], [pq])
                                        CP("dve", h32[ri].t[:, 0:16], pq.t[:, 0:16], [pq], [h32[ri]])
                                    ar = abre.t[:, ti:ti + 1]; ai = abim.t[:, ti:ti + 1]; nai = nabim.t[:, ti:ti + 1]
                                    for t in range(TS_):
                                        pv = slice(16 * t, 16 * t + 16); cu = slice(16 * t + 16, 16 * t + 32)
                                        STT(t5[0].t[:, 0:16], h32[0].t[:, pv], ar, pR.t[:, pv], ALU.mult, ALU.add, [h32[0], abre, pR], [t5[0]])
                                        STT(t5[1].t[:, 0:16], h32[1].t[:, pv], ar, pI_.t[:, pv], ALU.mult, ALU.add, [h32[1], abre, pI_], [t5[1]])
                                        STT(h32[0].t[:, cu], h32[1].t[:, pv], nai, t5[0].t[:, 0:16], ALU.mult, ALU.add, [h32[1], nabim, t5[0]], [h32[0]])
                                        STT(h32[1].t[:, cu], h32[0].t[:, pv], ai, t5[1].t[:, 0:16], ALU.mult, ALU.add, [h32[0], abim, t5[1]], [h32[1]])
                                    for ri, dsts in enumerate((sres, sims)):
                                        CP("act", hb[q][ri].t[:, 0:128], h32[ri].t[:, 16:144], [h32[ri]], [hb[q][ri]])
                                        pq = small()
                                        TR(pq.t[0:16, :], h32[ri].t[:, 128:144], id32[:], [h32[ri], id32], [pq])
                                        CP("dve", sst[:], pq.t[0:16, :], [pq], [sst])
                                        P.D("sp", dsts[0:NSM, ti * 128:(ti + 1) * 128], sst[:], r=[sst], is_output=True)
                            if so:
                                continue
                            p = big()
                            for q in range(4):
                                ti = fc * 4 + q
                                MM(p.t[:, 0:nb], Wc[0].t[:, ti, :], hb[q][0].t[:, 0:nb], [Wc[0], hb[q][0]], [p], start=(q == 0), stop=False, inc=False)
                                MM(p.t[:, 0:nb], Wc[1].t[:, ti, :], hb[q][1].t[:, 0:nb], [Wc[1], hb[q][1]], [p], start=False, stop=(q == 3), inc=(q == 3))
                            STT(yv.t[:, 0:nb], ubf.t[:, fc, 0:nb], dsk.t[:, fc:fc + 1], p.t[:, 0:nb], ALU.mult, ALU.add, [ubf, dsk, p], [yv])
                            TT(y2.t[:, 0:nb], yv.t[:, 0:nb], yv.t[:, 0:nb], ALU.mult, [yv], [y2])
                            TSC(y2.t[:, 0:nb], y2.t[:, 0:nb], 0.044715, 1.0, ALU.mult, ALU.add, [y2], [y2])
                            TT(y2.t[:, 0:nb], y2.t[:, 0:nb], yv.t[:, 0:nb], ALU.mult, [y2, yv], [y2])
                            ACT(y3.t[:, 0:nb], y2.t[:, 0:nb], AF.Sigmoid, [y2], [y3], scale=2.0 * math.sqrt(2.0 / math.pi))
                            TT(y5fm.t[:, fc, 0:nb], yv.t[:, 0:nb], y3.t[:, 0:nb], ALU.mult, [yv, y3], [y5fm])
                        ck(10)
                        if so:
                            P.fence()
                            continue
                        pss_ = pbig[3]
                        for j in range(8):
                            pa = proj(w_glu, j * 128, 128, nb, nk=8, src=y5fm)
                            pg = proj(w_glu, 1024 + j * 128, 128, nb, nk=8, src=y5fm)
                            ACT(y3.t[:, 0:nb], pg.t[:, 0:nb], AF.Sigmoid, [pg], [y3])
                            TT(yv.t[:, 0:nb], pa.t[:, 0:nb], y3.t[:, 0:nb], ALU.mult, [pa, y3], [yv])
                            CP("act", y5g.t[:, j, 0:nb], yv.t[:, 0:nb], [yv], [y5g])
                            y2b = y2.t[:, :].bitcast(BF16)[:, 0:nb]
                            TT(y2b, yv.t[:, 0:nb], yv.t[:, 0:nb], ALU.mult, [yv], [y2])
                            MM(pss_.t[:, 0:nb], onesb[:], y2b, [onesb, y2], [pss_], start=(j == 0), stop=(j == 7), inc=True)
                        TSC(y2.t[:, 0:nb], pss_.t[:, 0:nb], 1.0 / 1024, EPS, ALU.mult, ALU.add, [pss_], [y2])
                        ACT(y3.t[:, 0:nb], y2.t[:, 0:nb], AF.Sqrt, [y2], [y3])
                        RECIP(yv.t[:, 0:nb], y3.t[:, 0:nb], [y3], [yv])
                        for j in range(8):
                            STT(mixin.t[:, 8 + j, 0:nb], y5g.t[:, j, 0:nb], g5.t[:, j:j + 1], yv.t[:, 0:nb], ALU.mult, ALU.mult, [y5g, g5, yv], [mixin])
                        ck(11)
                        pass
                    P.fence()
                    with contextlib.ExitStack() as sc:
                        mixst = sb("mixst", [128, 4, D], F32, sc)
                        build_gtrow(32, seg)
                        ofm = [sb(f"ofm{i}", [128, 512], F32, sc) for i in range(2)]
                        for cb in range(16):
                            wb = load_w(w_out, 0, 16, cb * 128, 128)
                            pf_ = big()
                            for kc in range(KC):
                                MM(pf_.t[:, 0:nb], wb.t[:, kc, :], mixin.t[:, kc, 0:nb], [wb, mixin], [pf_],
                                   start=(kc == 0), stop=(kc == KC - 1), inc=(kc == KC - 1))
                            of_ = ofm[cb % 2]
                            CP("act", of_.t[:, 0:nb], pf_.t[:, 0:nb], [pf_], [of_])
                            for tl in range(ntile):
                                pt_ = small()
                                TR(pt_[:], of_.t[:, tl * 128:(tl + 1) * 128], id32[:], [of_, id32], [pt_])
                                CP(cpeng(), mixst.t[:, tl, cb * 128:(cb + 1) * 128], pt_[:], [pt_], [mixst])
                        for tl in range(ntile):
                            finish_residual(seg, Buf(mixst.t[:, tl, :], "mixv", trk=mixst), xsrc[c0 + tl * 128:c0 + (tl + 1) * 128, :], [], gtrow[seg.name],
                                            x1d.t[ydst_off + c0 + tl * 128:ydst_off + c0 + (tl + 1) * 128, :], [x1d], False)
                    P.fence()

            fbuf = sqb

            def finish_residual(seg, fsrc, xsrc_ap, xsrcbufs, gt, dst_ap, dstbufs, is_out):
                P.D("sp", xt[:], xsrc_ap, r=xsrcbufs, w=[xt])
                ACT(resb[:], fsrc[:], AF.Square, [fsrc], [resb])
                RSUM(cols4[0][:], resb[:], [resb], [cols4[0]])
                TSC(cols4[1][:], cols4[0][:], 1.0 / D, EPS, ALU.mult, ALU.add, [cols4[0]], [cols4[1]])
                ACT(cols4[2][:], cols4[1][:], AF.Sqrt, [cols4[1]], [cols4[2]])
                RECIP(cols4[3][:], cols4[2][:], [cols4[2]], [cols4[3]])
                STT(resb[:], fsrc[:], cols4[3].t[:, 0:1], gt[:], ALU.mult, ALU.mult, [fsrc, cols4[3], gt], [resb])
                TT(resb[:], resb[:], xt[:], ALU.add, [resb, xt], [resb])
                P.D("sp", dst_ap, resb[:], r=[resb], w=dstbufs, is_output=is_out)

            flg = sb("flg", [128, 1], F32, msc)
            P.D("sp", flg[:], flag, w=[flg])
            mixer(PREs, xpre, 0, so=True, first=True)
            ck(4)
            mixer(PRs, xp, 0, so=False, first=False)
            for ri, dsts in enumerate((srep, simp)):
                p = small()
                TR(p.t[0:32, :], hfin[ri][:], id32[:], [hfin[ri], id32], [p])
                CP("dve", stiny.t[0:32, :], p.t[0:32, :], [p], [stiny])
                P.D("sp", dsts, stiny.t[0:32, :], r=[stiny], is_output=True)
            ck(13)
            mixer(SMs, xs, TP)
            P.fence()
            msc.close()

            ck(14)
            with contextlib.ExitStack() as sc:
                hfm2 = sb("hfm2", [128, 16, 640], BF16, sc)
                act = sb("act", [128, FCH, 640], BF16, sc)
                wd = [sb(f"wd{i}", [128, 512], BF16, sc) for i in range(4)]
                sg = sb("sg", [128, 640], F32, sc)
                gtr2 = sb("gtr2", [128, D], F32, sc)
                wbig = [sb(f"wbig{i}", [128, 16, 512], BF16, sc) for i in range(2)]
                fst = [sb(f"fst{i}", [128, 512], F32, sc) for i in range(2)]
                build_gtrow(80, PRs)
                gtrow["sm"] = gtr2
                build_gtrow(80, SMs)
                tiles = [(PRs, tl * 128, yp, tl * 128) for tl in range(TP // 128)] + [(SMs, TP, ys, 0)]
                nwb = [0]

                def load_wbig(W, c0):
                    nwb[0] += 1
                    b_ = wbig[nwb[0] % 2]
                    P.D("pool", b_.t[:, :, :], W[0:D, c0:c0 + 512].rearrange("(k p) n -> p k n", p=128), w=[b_])
                    return b_

                for blk_tiles in (tiles[0:5], tiles[5:9]):
                    nt = len(blk_tiles)
                    nb = nt * 128
                    for i, (seg, r0, yd, y0) in enumerate(blk_tiles):
                        make_h(seg, x1d.t[r0:r0 + 128, :], [x1d], 64, 48, i * 128, hdst=hfm2)
                    halves = [(0, nb // 2), (nb // 2, nb)] if nb > 512 else [(0, nb)]
                    for f4 in range(FCH // 4):
                        wg = load_wbig(w_gate, f4 * 512)
                        wu = load_wbig(w_up, f4 * 512)
                        for fi in range(4):
                            f = f4 * 4 + fi
                            for (a_, b_) in halves:
                                n_ = b_ - a_
                                pg = big()
                                for kc in range(KC):
                                    MM(pg.t[:, 0:n_], wg.t[:, kc, fi * 128:(fi + 1) * 128], hfm2.t[:, kc, a_:b_], [wg, hfm2], [pg],
                                       start=(kc == 0), stop=(kc == KC - 1), inc=(kc == KC - 1))
                                pu = big()
                                for kc in range(KC):
                                    MM(pu.t[:, 0:n_], wu.t[:, kc, fi * 128:(fi + 1) * 128], hfm2.t[:, kc, a_:b_], [wu, hfm2], [pu],
                                       start=(kc == 0), stop=(kc == KC - 1), inc=(kc == KC - 1))
                                ACT(sg.t[:, a_:b_], pg.t[:, 0:n_], AF.Silu, [pg], [sg])
                                TT(act.t[:, f, a_:b_], sg.t[:, a_:b_], pu.t[:, 0:n_], ALU.mult, [sg, pu], [act])
                    banks = [pbig[0], pbig[1], pbig[2], pbig[3], psm_t[0]]
                    for cb in range(4):
                        pbs = banks[:nt]
                        for f in range(FCH):
                            w_ = wd[f % 4]
                            P.D("pool", w_[:], w_down[f * 128:(f + 1) * 128, cb * 512:(cb + 1) * 512], w=[w_])
                            for tl in range(nt):
                                MM(pbs[tl][:], act.t[:, f, tl * 128:(tl + 1) * 128], w_[:], [act, w_], [pbs[tl]],
                                   start=(f == 0), stop=(f == FCH - 1), inc=True)
                        for tl, (seg, r0, yd, y0) in enumerate(blk_tiles):
                            fs_ = fst[tl % 2]
                            CP(cpeng(), fs_[:], pbs[tl][:], [pbs[tl]], [fs_])
                            P.D("sp", fd.t[r0:r0 + 128, cb * 512:(cb + 1) * 512], fs_[:], r=[fs_], w=[fd])
                    for i, (seg, r0, yd, y0) in enumerate(blk_tiles):
                        P.D("sp", sqb[:], fd.t[r0:r0 + 128, :], r=[fd], w=[sqb])
                        finish_residual(seg, sqb, x1d.t[r0:r0 + 128, :], [x1d], gtrow[seg.name],
                                        yd[y0:y0 + 128, :], [], True)

        except _Stop:
            pass
        DEAD[0] = False
        P.finish()
        with nc.Block() as block:
            P.emit(block)
    return nc, P


_CACHE = {}


def _consts():
    idn = np.eye(128, dtype=np.float32)
    j = np.arange(128)[:, None]; i = np.arange(128)[None, :]
    mneg = np.where(i >= j, 0.0, -30000.0).astype(np.float32)
    strict = (i > j).astype(np.float32)
    sel = np.zeros((8, 1024), np.float32)
    for h in range(8):
        sel[h, h * 128:(h + 1) * 128] = 1.0
    cm = np.zeros((128, 4, 128), np.float32)
    for q in range(4):
        for g2 in range(2):
            g8 = 2 * q + g2
            cm[g8 * 16:(g8 + 1) * 16, q, g2 * 64:(g2 + 1) * 64] = 1.0
    return idn, mneg, strict, sel, cm.reshape(128, 512)


def _bmask():
    a = np.arange(128) // 32
    return (a[:, None] == a[None, :]).astype(np.float32)


def kernel(x_prompt, x_sample, c_prompt, c_sample, state_conv, state_delta, state_ssm_re, state_ssm_im,
           w_ada, b_ada, g_pre_mix, g_post_mix, g_pre_ffn, g_post_ffn,
           w_in, w_conv, a_log, dt_bias, g_dn_out,
           lam_re, lam_im, log_step, b_re, b_im, c_re, c_im, d_skip,
           w_glu, g_s5_out, w_out, w_gate, w_up, w_down):
    f = lambda a: np.ascontiguousarray(np.asarray(a, dtype=np.float32))
    if "nc" not in _CACHE:
        _CACHE["nc"] = build_program()[0]
    nc = _CACHE["nc"]
    idn, mneg, strict, sel, cm = _consts()
    shared = {
        "w_ada": f(w_ada[0]), "b_ada": f(b_ada[0]).reshape(96, 128),
        "gains": f(np.concatenate([g_pre_mix[0], g_post_mix[0], g_pre_ffn[0], g_post_ffn[0]])).reshape(64, 128),
        "w_in": f(w_in[0]), "w_conv": f(w_conv[0]).reshape(96, 128),
        "hv": f(np.stack([a_log[0], dt_bias[0]], axis=1)), "gdn": f(g_dn_out[0]).reshape(1, 128),
        "lam_re": f(lam_re[0]).reshape(32, 128), "lam_im": f(lam_im[0]).reshape(32, 128),
        "lstep": f(log_step[0]).reshape(32, 2),
        "b_re": f(b_re[0]).reshape(4096, 16), "b_im": f(b_im[0]).reshape(4096, 16),
        "c_re": f(c_re[0]).reshape(1024, 64), "c_im": f(c_im[0]).reshape(1024, 64),
        "dskip": f(d_skip[0]).reshape(8, 128), "gs5": f(g_s5_out[0]).reshape(8, 128),
        "w_glu": f(w_glu[0]), "w_out": f(w_out[0]), "w_gate": f(w_gate[0]), "w_up": f(w_up[0]), "w_down": f(w_down[0]),
        "k_id": idn, "k_mneg": mneg, "k_strict": strict, "k_sel": sel, "k_cmask": cm, "k_bmask": _bmask(),
    }
    in_maps = []
    for c in range(8):
        sq = slice(16 * c, 16 * c + 16)
        m = dict(shared)
        sq_, hf_ = c // 2, c % 2
        m["xp"] = f(x_prompt[sq_, hf_ * TP:(hf_ + 1) * TP])
        m["xpre"] = f(x_prompt[sq_, 0:TP])
        m["flag"] = np.full((128, 1), float(hf_), np.float32)
        m["xs"] = f(np.transpose(x_sample[sq], (1, 0, 2)).reshape(128, D))
        m["call"] = f(np.concatenate([c_prompt[c // 2][None], c_sample[sq]], axis=0))
        m["sconv"] = f(np.transpose(state_conv[0][sq], (1, 0, 2)).reshape(48, QKV))
        m["sdelta"] = f(state_delta[0][sq])
        m["sre"] = f(state_ssm_re[0][sq]).reshape(16, 4096)
        m["sim"] = f(state_ssm_im[0][sq]).reshape(16, 4096)
        in_maps.append(m)
    res = run_bass_kernel_spmd(nc, in_maps, core_ids=list(range(8)))
    R = res.results
    yp = np.stack([np.concatenate([R[2 * b]["yp"], R[2 * b + 1]["yp"]], axis=0) for b in range(4)], axis=0)
    ys = np.concatenate([R[c]["ys"].reshape(8, 16, D).transpose(1, 0, 2) for c in range(8)], axis=0)
    convp = np.stack([R[2 * b + 1]["convp"] for b in range(4)], axis=0)[None]
    deltap = np.stack([R[2 * b + 1]["deltap"] for b in range(4)], axis=0)[None]
    srep = np.stack([R[2 * b + 1]["srep"].reshape(64, 64) for b in range(4)], axis=0)[None]
    simp = np.stack([R[2 * b + 1]["simp"].reshape(64, 64) for b in range(4)], axis=0)[None]
    convs = np.concatenate([R[c]["convs"].reshape(3, 16, QKV).transpose(1, 0, 2) for c in range(8)], axis=0)[None]
    deltas = np.concatenate([R[c]["deltas"] for c in range(8)], axis=0)[None]
    sres = np.concatenate([R[c]["sres"].reshape(16, 64, 64) for c in range(8)], axis=0)[None]
    sims = np.concatenate([R[c]["sims"].reshape(16, 64, 64) for c in range(8)], axis=0)[None]
    outs = (yp, ys, convp, deltap, srep, simp, convs, deltas, sres, sims)
    return tuple(np.ascontiguousarray(o.astype(np.float32)) for o in outs)
```
